# Optimizing a Trainium2 kernel written in Bass

```python
import math
import jax, jax.numpy as jnp
from jax import lax
import numpy as np

D_MODEL = 1024
BATCH = 8
SEQ = 2048
DEPTH = 2

HEAD_DIM = 128
HEADS_PER_GROUP = 4
ATT_GROUPS = ((128, 1), (512, 4), (2048, 16))
N_GROUPS = len(ATT_GROUPS)
ATT_WIDTH = N_GROUPS * HEADS_PER_GROUP * HEAD_DIM
ATT_OUT = HEADS_PER_GROUP * HEAD_DIM
ROPE_DIM = HEAD_DIM // 4
ROPE_THETA = 500000.0
NEG_INF = -1e30
HYENA_WIDTH = D_MODEL
HYENA_ORDER = 2
SHORT_CONV = 3
FILTER_EMB = 33
FILTER_HIDDEN = 64
FILTER_INIT_SCALE = 0.05
DECAY_TARGET = 1e-2
FAST_DECAY_PCT = 0.3
SLOW_DECAY_PCT = 1.5
IN_WIDTH = 3 * ATT_WIDTH + 3 * HYENA_WIDTH
N_BRANCH = 2
D_FF = ((8 * D_MODEL // 3 + 255) // 256) * 256
RMS_EPS = 1e-6

kernel_name = "hybrid_dilated_attn_hyena_block"


def rms_norm(x, g):
    xf = x.astype(jnp.float32)
    y = xf * lax.rsqrt(jnp.mean(xf * xf, axis=-1, keepdims=True) + RMS_EPS)
    return (y * g.astype(jnp.float32)).astype(x.dtype)


def partial_rotary(x, pos):
    S = x.shape[1]
    half = ROPE_DIM // 2
    inv = ROPE_THETA ** (-jnp.arange(0, ROPE_DIM, 2, dtype=jnp.float32) / ROPE_DIM)
    ang = pos.astype(jnp.float32)[:, None] * inv[None, :]
    bshape = (1, S) + (1,) * (x.ndim - 3) + (half,)
    cos, sin = jnp.cos(ang).reshape(bshape), jnp.sin(ang).reshape(bshape)
    xr = x[..., :ROPE_DIM].astype(jnp.float32)
    x1, x2 = xr[..., :half], xr[..., half:]
    rot = jnp.concatenate([x1 * cos - x2 * sin, x2 * cos + x1 * sin], axis=-1).astype(x.dtype)
    return jnp.concatenate([rot, x[..., ROPE_DIM:]], axis=-1)


def dilated_window_attention(q, k, v, dilation, nbr):
    B, S, H, hd = q.shape
    Ls = S // dilation
    nb = -(-Ls // nbr)
    Lp = nb * nbr

    def to_sub(t):
        t = t.reshape(B, Ls, dilation, H, hd).transpose(0, 2, 3, 1, 4)
        t = jnp.pad(t, ((0, 0), (0, 0), (0, 0), (0, Lp - Ls), (0, 0)))
        return t.reshape(B, dilation, H, nb, nbr, hd).astype(jnp.float32)

    qs, ks, vs = to_sub(q), to_sub(k), to_sub(v)

    def band(t):
        z = jnp.zeros_like(t[:, :, :, :1])
        tp = jnp.concatenate([z, t, z], axis=3)
        return jnp.concatenate([tp[:, :, :, :-2], tp[:, :, :, 1:-1], tp[:, :, :, 2:]], axis=4)

    kb, vb = band(ks), band(vs)
    qi = jnp.arange(nb)[:, None] * nbr + jnp.arange(nbr)[None, :]
    kj = (jnp.arange(nb)[:, None] - 1) * nbr + jnp.arange(3 * nbr)[None, :]
    valid = ((jnp.abs(qi[:, :, None] - kj[:, None, :]) <= nbr)
             & (kj[:, None, :] >= 0) & (kj[:, None, :] < Ls))
    s = jnp.einsum('brhnqd,brhnkd->brhnqk', qs, kb) * (hd ** -0.5)
    s = jnp.where(valid, s, NEG_INF)
    lse = jax.nn.logsumexp(s, axis=-1)
    p = jnp.exp(s - lse[..., None])
    o = jnp.einsum('brhnqk,brhnkd->brhnqd', p, vb)
    o = o.reshape(B, dilation, H, Lp, hd)[:, :, :, :Ls].transpose(0, 3, 1, 2, 4).reshape(B, S, H, hd)
    lse = lse.reshape(B, dilation, H, Lp)[..., :Ls].transpose(0, 3, 1, 2).reshape(B, S, H)
    return o, lse


def short_conv(u, w, b):
    C = u.shape[-1]
    y = lax.conv_general_dilated(u, w[:, None, :].astype(u.dtype), window_strides=(1,), padding='SAME',
                                 dimension_numbers=('NWC', 'WIO', 'NWC'), feature_group_count=C)
    return y + b.astype(u.dtype)


def filter_features(L):
    bands = (FILTER_EMB - 1) // 2
    t = jnp.linspace(0.0, 1.0, L, dtype=jnp.float32)[:, None]
    w = 2.0 * math.pi * jnp.arange(L, dtype=jnp.float32)[:, None] / L
    f = jnp.linspace(1e-4, bands - 1, bands, dtype=jnp.float32)[None, :]
    return jnp.concatenate([t, jnp.cos(f * w), -jnp.sin(f * w)], axis=-1), t[:, 0]


def implicit_filters(L, w1, b1, fr1, w2, b2, fr2, w3):
    z, t = filter_features(L)
    h = jnp.sin(fr1.astype(jnp.float32) * (z @ w1.astype(jnp.float32) + b1.astype(jnp.float32)))
    h = jnp.sin(fr2.astype(jnp.float32) * (h @ w2.astype(jnp.float32) + b2.astype(jnp.float32)))
    h = (h @ w3.astype(jnp.float32)).reshape(L, 2 * HYENA_ORDER, HYENA_WIDTH).transpose(1, 2, 0)
    deltas = jnp.linspace(math.log(DECAY_TARGET) / SLOW_DECAY_PCT, math.log(DECAY_TARGET) / FAST_DECAY_PCT,
                          HYENA_WIDTH, dtype=jnp.float32)
    decay = jnp.exp(-t[None, :] * jnp.abs(deltas)[:, None])
    return h * decay[None]


def bidir_fft_conv(z, h_fwd, h_bwd, bias):
    B, L, C = z.shape
    k = jnp.concatenate([h_fwd[:, :1] + h_bwd[:, :1], h_fwd[:, 1:], jnp.zeros_like(h_fwd[:, :1]),
                         h_bwd[:, :0:-1]], axis=-1)
    kf = jnp.fft.rfft(k, axis=-1).T
    zf32 = z.astype(jnp.float32)
    zf = jnp.fft.rfft(zf32, n=2 * L, axis=1)
    y = jnp.fft.irfft(zf * kf[None], n=2 * L, axis=1)[:, :L]
    return (y + zf32 * bias.astype(jnp.float32)).astype(z.dtype)


def hybrid_mixer(u, pos, w_in, conv_w, conv_b, fw1, fb1, ff1, fw2, fb2, ff2, fw3, hy_bias,
                 w_o_attn, w_o_hyena, w_gate, b_gate, w_out):
    B, S, _ = u.shape
    proj = u @ w_in
    qkv = proj[..., :3 * ATT_WIDTH].reshape(B, S, 3, N_GROUPS, HEADS_PER_GROUP, HEAD_DIM)
    q = partial_rotary(qkv[:, :, 0], pos)
    k = partial_rotary(qkv[:, :, 1], pos)
    v = qkv[:, :, 2]
    outs, lses = [], []
    for g, (window, dil) in enumerate(ATT_GROUPS):
        o, l = dilated_window_attention(q[:, :, g], k[:, :, g], v[:, :, g], dil, window // (2 * dil))
        outs.append(o)
        lses.append(l)
    alpha = jax.nn.softmax(jnp.stack(lses, axis=2), axis=2)
    o_att = jnp.einsum('bsgh,bsghd->bshd', alpha, jnp.stack(outs, axis=2)).reshape(B, S, ATT_OUT)
    y_att = o_att.astype(u.dtype) @ w_o_attn
    hz = short_conv(proj[..., 3 * ATT_WIDTH:], conv_w, conv_b)
    hv, hx1, hx2 = jnp.split(hz, 3, axis=-1)
    filt = implicit_filters(S, fw1, fb1, ff1, fw2, fb2, ff2, fw3)
    z = hx1 * bidir_fft_conv(hv, filt[0], filt[1], hy_bias[0])
    z = hx2 * bidir_fft_conv(z, filt[2], filt[3], hy_bias[1])
    y_hy = z @ w_o_hyena
    gates = jax.nn.sigmoid((u @ w_gate + b_gate).astype(jnp.float32)).reshape(B, S, N_BRANCH, D_MODEL)
    merged = gates[:, :, 0] * y_att.astype(jnp.float32) + gates[:, :, 1] * y_hy.astype(jnp.float32)
    return merged.astype(u.dtype) @ w_out


def swiglu(u, w_gu, w_down):
    a, b = jnp.split(u @ w_gu, 2, axis=-1)
    return (jax.nn.silu(a) * b) @ w_down


def setup_inputs(seed: int = 0) -> dict:
    key = jax.random.key(seed)
    ks = jax.random.split(key, 24)

    def nrm(k, shape, scale):
        return jax.random.normal(k, shape, jnp.float32) * scale

    def gain(k):
        return 1.0 + nrm(k, (DEPTH, D_MODEL), 0.05)

    return {
        "x": nrm(ks[0], (BATCH, SEQ, D_MODEL), 1.0),
        "norm_mix_pre": gain(ks[1]),
        "norm_mix_post": gain(ks[2]),
        "norm_ffn_pre": gain(ks[3]),
        "norm_ffn_post": gain(ks[4]),
        "w_in": nrm(ks[5], (DEPTH, D_MODEL, IN_WIDTH), D_MODEL ** -0.5),
        "conv_w": nrm(ks[6], (DEPTH, SHORT_CONV, 3 * HYENA_WIDTH), SHORT_CONV ** -0.5),
        "conv_b": nrm(ks[7], (DEPTH, 3 * HYENA_WIDTH), 0.02),
        "filt_w1": nrm(ks[8], (DEPTH, FILTER_EMB, FILTER_HIDDEN), FILTER_EMB ** -0.5),
        "filt_b1": nrm(ks[9], (DEPTH, FILTER_HIDDEN), 0.02),
        "filt_freq1": 1.0 + nrm(ks[10], (DEPTH, FILTER_HIDDEN), 0.1),
        "filt_w2": nrm(ks[11], (DEPTH, FILTER_HIDDEN, FILTER_HIDDEN), FILTER_HIDDEN ** -0.5),
        "filt_b2": nrm(ks[12], (DEPTH, FILTER_HIDDEN), 0.02),
        "filt_freq2": 1.0 + nrm(ks[13], (DEPTH, FILTER_HIDDEN), 0.1),
        "filt_w3": nrm(ks[14], (DEPTH, FILTER_HIDDEN, 2 * HYENA_ORDER * HYENA_WIDTH),
                       FILTER_HIDDEN ** -0.5 * FILTER_INIT_SCALE),
        "hyena_bias": nrm(ks[15], (DEPTH, HYENA_ORDER, HYENA_WIDTH), 1.0),
        "w_o_attn": nrm(ks[16], (DEPTH, ATT_OUT, D_MODEL), ATT_OUT ** -0.5),
        "w_o_hyena": nrm(ks[17], (DEPTH, HYENA_WIDTH, D_MODEL), HYENA_WIDTH ** -0.5),
        "w_gate": nrm(ks[18], (DEPTH, D_MODEL, N_BRANCH * D_MODEL), D_MODEL ** -0.5),
        "b_gate": nrm(ks[19], (DEPTH, N_BRANCH * D_MODEL), 0.02),
        "w_out": nrm(ks[20], (DEPTH, D_MODEL, D_MODEL), D_MODEL ** -0.5),
        "w_gate_up": nrm(ks[21], (DEPTH, D_MODEL, 2 * D_FF), D_MODEL ** -0.5),
        "w_down": nrm(ks[22], (DEPTH, D_FF, D_MODEL), D_FF ** -0.5),
    }


def reference(x, norm_mix_pre, norm_mix_post, norm_ffn_pre, norm_ffn_post, w_in, conv_w, conv_b,
              filt_w1, filt_b1, filt_freq1, filt_w2, filt_b2, filt_freq2, filt_w3, hyena_bias,
              w_o_attn, w_o_hyena, w_gate, b_gate, w_out, w_gate_up, w_down):
    pos = jnp.arange(x.shape[1], dtype=jnp.int32)
    for l in range(DEPTH):
        u = rms_norm(x, norm_mix_pre[l])
        m = hybrid_mixer(u, pos, w_in[l], conv_w[l], conv_b[l], filt_w1[l], filt_b1[l], filt_freq1[l],
                         filt_w2[l], filt_b2[l], filt_freq2[l], filt_w3[l], hyena_bias[l],
                         w_o_attn[l], w_o_hyena[l], w_gate[l], b_gate[l], w_out[l])
        x = x + rms_norm(m, norm_mix_post[l])
        u = rms_norm(x, norm_ffn_pre[l])
        x = x + rms_norm(swiglu(u, w_gate_up[l], w_down[l]), norm_ffn_post[l])
    return x
```

```python
import math
from contextlib import ExitStack

import numpy as np
import ml_dtypes
import concourse.bass as bass
import concourse.mybir as mybir
from concourse.bass_utils import run_bass_kernel_spmd

F32 = mybir.dt.float32
BF16 = mybir.dt.bfloat16
AF = mybir.ActivationFunctionType
ALU = mybir.AluOpType

NT = 2048
DM = 1024
DFF = 2816
EPS = 1e-6
GROUPS = ((1, 16), (4, 4), (16, 1))


class Buf:
    def __init__(self, t, name):
        self.t = t
        self.name = name
        self.writers = {}
        self.readers = {}
        self.dsem = None
        self.dcount = 0
        self.is_psum = False

    def __getitem__(self, idx):
        return self.t[idx]


class Rot:
    def __init__(self, bufs):
        self.b = list(bufs)
        self.i = 0

    def next(self):
        b = self.b[self.i % len(self.b)]
        self.i += 1
        return b


class Prog:
    def __init__(self, needed=None):
        self.needed = needed
        self.needed_set = {e: set(v) for e, v in needed.items()} if needed else None
        self.rec = {}
        self.nc = bass.Bass("TRN2", target_bir_lowering=False)
        self.es = ExitStack()
        nc = self.nc
        self.eng = {"pe": nc.tensor, "act": nc.scalar, "dve": nc.vector, "pool": nc.gpsimd, "sp": nc.sync}
        self.sem = {}
        self.cnt = {}
        for e in ("pe", "act", "dve", "pool"):
            self.sem[e] = self.es.enter_context(nc.semaphore("s_" + e))
            self.cnt[e] = 0
        self.seen = {e: {} for e in self.eng}
        self.semeng = {id(self.sem[e]): e for e in self.sem}
        self.dsems = {}
        self.dsem_pool = []
        self.nbuf = 0

    def sbuf(self, shape, dtype, st=None, name=None):
        self.nbuf += 1
        name = name or f"sb{self.nbuf}"
        t = (st.st if st else self.es).enter_context(self.nc.sbuf_tensor(name, list(shape), dtype))
        b = Buf(t, name)
        if st:
            st.bufs.append(b)
        return b

    def psum(self, shape, dtype=F32, st=None, name=None):
        self.nbuf += 1
        name = name or f"ps{self.nbuf}"
        t = (st.st if st else self.es).enter_context(self.nc.psum_tensor(name, list(shape), dtype))
        b = Buf(t, name)
        b.is_psum = True
        if st:
            st.bufs.append(b)
        return b

    def dram(self, name, shape, dtype, kind="Internal"):
        t = self.nc.dram_tensor(name, list(shape), dtype, kind=kind).ap()
        return Buf(t, name)

    def _dsem(self, buf):
        if buf.dsem is None:
            if self.dsem_pool:
                buf.dsem, buf.dcount = self.dsem_pool.pop()
            else:
                buf.dsem = self.es.enter_context(self.nc.semaphore("d_" + buf.name))
        return buf.dsem

    def _wait(self, e, sem, val):
        k = id(sem)
        if self.seen[e].get(k, 0) >= val:
            return
        self.seen[e][k] = val
        se = self.semeng.get(k)
        if se is not None:
            self.rec.setdefault(se, set()).add(val)
            if self.needed is not None:
                import bisect
                val = bisect.bisect_right(self.needed[se], val)
        self.eng[e].wait_ge(sem, val)

    def _deps(self, e, reads, writes):
        need = {}
        for b in reads:
            for k, (s, v) in b.writers.items():
                if need.get(k, (None, 0))[1] < v:
                    need[k] = (s, v)
            if b.is_psum:
                own = id(self.sem[e]) if e in self.sem else None
                for k, (s, v) in b.readers.items():
                    if k != own and need.get(k, (None, 0))[1] < v:
                        need[k] = (s, v)
        for b in writes:
            for d in (b.writers, b.readers):
                for k, (s, v) in d.items():
                    if need.get(k, (None, 0))[1] < v:
                        need[k] = (s, v)
        for k, (s, v) in need.items():
            if e == "pe" and s is self.sem["pe"]:
                continue
            self._wait(e, s, v)

    def _record(self, ev, reads, writes):
        s, v = ev
        k = id(s)
        for b in writes:
            b.readers = {}
            b.writers[k] = (s, v)
        for b in reads:
            if b in writes:
                continue
            b.readers[k] = (s, v)

    def op(self, e, fn, reads=(), writes=()):
        reads = [b for b in reads if b is not None]
        writes = [b for b in writes if b is not None]
        self._deps(e, reads, writes)
        ins = fn(self.eng[e])
        self.cnt[e] += 1
        if self.needed_set is None or self.cnt[e] in self.needed_set.get(e, ()):
            ins.then_inc(self.sem[e], 1)
        self._record((self.sem[e], self.cnt[e]), reads, writes)
        return ins

    def dma(self, dst_buf, dst_ap, src_buf, src_ap, q="sp", **kw):
        self._deps(q, [src_buf], [dst_buf])
        sem = self._dsem(dst_buf)
        ins = self.eng[q].dma_start(out=dst_ap, in_=src_ap, **kw)
        dst_buf.dcount += 16
        ins.then_inc(sem, 16)
        self.dsems[id(sem)] = (sem, dst_buf.dcount)
        self._record((sem, dst_buf.dcount), [src_buf], [dst_buf])
        return ins

    def barrier(self):
        for e in self.eng:
            for e2 in self.sem:
                if e2 != e and self.cnt[e2] > 0:
                    self._wait(e, self.sem[e2], self.cnt[e2])
            for sm, v in self.dsems.values():
                self._wait(e, sm, v)

    def finish(self):
        self.barrier()
        self.es.close()
        return self.nc


class Phase:
    def __init__(self, p):
        self.p = p
        self.st = ExitStack()
        self.bufs = []

    def __enter__(self):
        self.st.__enter__()
        return self

    def __exit__(self, *a):
        self.p.barrier()
        for b in self.bufs:
            if b.dsem is not None:
                self.p.dsem_pool.append((b.dsem, b.dcount))
                b.dsem = None
        return self.st.__exit__(*a)


def _bf(a):
    return np.ascontiguousarray(a.astype(ml_dtypes.bfloat16))


def make_consts():
    c = {}
    c["ident"] = _bf(np.eye(128, dtype=np.float32))
    c["ones"] = _bf(np.ones((128, 128), dtype=np.float32))
    inv = (500000.0 ** (-np.arange(0, 32, 2, dtype=np.float32) / 32)).astype(np.float32)
    pos = np.arange(NT, dtype=np.float32)
    ang = pos[:, None] * inv[None, :]
    cs, sn = np.cos(ang).astype(np.float32), np.sin(ang).astype(np.float32)
    rope = np.zeros((3, 128, 16, 32), np.float32)
    for g, (D, T) in enumerate(GROUPS):
        for jt in range(16):
            r, a = jt // T, jt % T
            tok = r + D * (128 * a + np.arange(128))
            rope[g, :, jt, 0:16] = cs[tok]
            rope[g, :, jt, 16:32] = sn[tok]
    c["rope"] = rope
    pp = np.arange(128)[:, None]
    qq = np.arange(128)[None, :]
    lo = pp < 64
    A = np.where(lo, (qq - pp) <= 64, (pp - qq) >= 64)
    B = np.where(lo, (qq - pp) >= 64, (pp - qq) <= 64)
    Af = A & lo
    Bl = B & (~lo)
    C = np.abs(pp - qq) <= 64
    c["masks"] = _bf(np.stack([A, Af, B, Bl, C], axis=1).astype(np.float32))
    L = NT
    t = np.linspace(0.0, 1.0, L, dtype=np.float32)[:, None]
    w = (2.0 * math.pi * np.arange(L, dtype=np.float32)[:, None] / L).astype(np.float32)
    f = np.linspace(1e-4, 15, 16, dtype=np.float32)[None, :]
    z = np.concatenate([t, np.cos(f * w), -np.sin(f * w)], axis=-1).astype(np.float32)
    c["zT"] = np.ascontiguousarray(z.T)
    deltas = np.linspace(math.log(1e-2) / 1.5, math.log(1e-2) / 0.3, 1024, dtype=np.float32)
    dec = np.exp(-t * np.abs(deltas)[None, :]).astype(np.float32)
    c["decay"] = np.ascontiguousarray(dec.reshape(16, 128, 1024).transpose(1, 0, 2))
    n = np.arange(2048, dtype=np.float64)
    ph = 2.0 * np.pi * (n[:, None] * (n[None, :] + 0.5)) / 4096.0
    Fc, Fs = np.cos(ph), np.sin(ph)
    def fwd(M):
        return _bf(M.reshape(16, 128, 16, 128).transpose(2, 1, 0, 3).astype(np.float32))
    c["ffc"] = fwd(Fc)
    c["ffs"] = fwd(Fs)
    def inv_(M):
        MT = (M.T / 2048.0).reshape(16, 128, 4, 512)
        return MT.transpose(2, 1, 0, 3)
    c["finv"] = _bf(np.concatenate([inv_(Fc), inv_(Fs)], axis=2).astype(np.float32))
    return c


CONST_SHAPES = {
    "ident": ([128, 128], BF16), "ones": ([128, 128], BF16), "rope": ([3, 128, 16, 32], F32),
    "masks": ([128, 5, 128], BF16), "zT": ([33, 2048], F32), "decay": ([128, 16, 1024], F32),
    "ffc": ([16, 128, 16, 128], BF16), "ffs": ([16, 128, 16, 128], BF16),
    "finv": ([4, 128, 32, 512], BF16),
}

IN_SHAPES = {
    "x": [NT, DM], "norm_mix_pre": [2, DM], "norm_mix_post": [2, DM], "norm_ffn_pre": [2, DM],
    "norm_ffn_post": [2, DM], "w_in": [2, DM, 7680], "conv_w": [2, 3, 3072], "conv_b": [2, 3072],
    "filt_w1": [2, 33, 64], "filt_b1": [2, 64], "filt_freq1": [2, 64], "filt_w2": [2, 64, 64],
    "filt_b2": [2, 64], "filt_freq2": [2, 64], "filt_w3": [2, 64, 4096], "hyena_bias": [2, 2, 1024],
    "w_o_attn": [2, 512, DM], "w_o_hyena": [2, DM, DM], "w_gate": [2, DM, 2048], "b_gate": [2, 2048],
    "w_out": [2, DM, DM], "w_gate_up": [2, DM, 2 * DFF], "w_down": [2, DFF, DM],
}


def load_cast(p, stg, dst_buf, dst_ap, src_buf, src_ap, view, eng="pool"):
    s = stg.next()
    sv = view(s)
    p.dma(s, sv, src_buf, src_ap)
    p.op(eng, lambda e: e.tensor_copy(out=dst_ap, in_=sv), [s], [dst_buf])


def rstd_from_ss(p, ss, rstd):
    p.op("dve", lambda e: e.tensor_scalar(out=rstd[:], in0=ss[:], scalar1=1.0 / DM, scalar2=EPS,
                                          op0=ALU.mult, op1=ALU.add), [ss], [rstd])
    p.op("act", lambda e: e.activation(out=rstd[:], in_=rstd[:], func=AF.Sqrt), [rstd], [rstd])
    p.op("dve", lambda e: e.reciprocal(out=rstd[:], in_=rstd[:]), [rstd], [rstd])


def norm_chain(p, src, src_ap, ss, rstd, junk, u0):
    p.op("act", lambda e: e.activation(out=junk[:], in_=src_ap, func=AF.Square, accum_out=ss[:]),
         [src], [junk, ss])
    rstd_from_ss(p, ss, rstd)
    p.op("act", lambda e: e.activation(out=u0[:], in_=src_ap, func=AF.Copy, scale=rstd[:, 0:1]),
         [src, rstd], [u0])


def norm_T(p, u0, pst, ident, gB, uT, t):
    ptv = pst[:].bitcast(BF16)
    for kc in range(8):
        p.op("pe", lambda e: e.transpose(ptv[:, kc * 128:(kc + 1) * 128], u0[:, kc * 128:(kc + 1) * 128],
                                         ident[:]), [u0, ident], [pst])
    p.op("dve", lambda e: e.tensor_tensor(out=uT[:, :, t * 128:(t + 1) * 128],
                                          in0=ptv.rearrange("p (a b) -> p a b", a=8), in1=gB[:],
                                          op=ALU.mult), [pst, gB], [uT])


def load_gB(p, st, gvec_buf, gvec_ap):
    gT = p.sbuf([128, 8], F32, st)
    p.dma(gT, gT[:], gvec_buf, gvec_ap.rearrange("(c p) -> p c", p=128), allow_slow_non_contiguous=True)
    gB = p.sbuf([128, 8, 128], BF16, st)
    p.op("dve", lambda e: e.tensor_copy(out=gB[:], in_=gT[:].unsqueeze(2).to_broadcast([128, 8, 128])),
         [gT], [gB])
    return gB


def phase_norm(p, C, x_d, g_buf, g_ap, uT_d):
    with Phase(p) as st:
        ident = p.sbuf([128, 128], BF16, st)
        p.dma(ident, ident[:], C["ident"], C["ident"][:])
        gB = load_gB(p, st, g_buf, g_ap)
        uT = p.sbuf([128, 8, NT], BF16, st)
        xs = Rot([p.sbuf([128, DM], F32, st) for _ in range(3)])
        u0s = Rot([p.sbuf([128, DM], BF16, st) for _ in range(2)])
        junks = Rot([p.sbuf([128, DM], BF16, st) for _ in range(2)])
        sss = Rot([p.sbuf([128, 1], F32, st) for _ in range(3)])
        rstds = Rot([p.sbuf([128, 1], F32, st) for _ in range(3)])
        pss = Rot([p.psum([128, 512], F32, st) for _ in range(2)])
        for t in range(16):
            xb = xs.next()
            p.dma(xb, xb[:], x_d, x_d[t * 128:(t + 1) * 128, :])
            u0 = u0s.next()
            norm_chain(p, xb, xb[:], sss.next(), rstds.next(), junks.next(), u0)
            norm_T(p, u0, pss.next(), ident, gB, uT, t)
        p.dma(uT_d, uT_d[:], uT, uT[:])


def phase_attn(p, C, W, l, uT_d, oT_d, norm=None, uT_sb=None):
    with Phase(p) as st:
        ident = p.sbuf([128, 128], BF16, st)
        p.dma(ident, ident[:], C["ident"], C["ident"][:])
        ones = p.sbuf([128, 128], BF16, st)
        p.dma(ones, ones[:], C["ones"], C["ones"][:])
        masks = p.sbuf([128, 5, 128], BF16, st)
        p.dma(masks, masks[:], C["masks"], C["masks"][:])
        rope = p.sbuf([128, 3, 16, 32], F32, st)
        p.dma(rope, rope[:], C["rope"], C["rope"][:].rearrange("g p t c -> p g t c"))
        uT = uT_sb if uT_sb is not None else p.sbuf([128, 8, NT], BF16, st)
        if norm is None:
            p.dma(uT, uT[:], uT_d, uT_d[:])
        wq = p.sbuf([128, 8, 1536], BF16, st)
        stg = Rot([p.sbuf([128, 2, 512], F32, st) for _ in range(2)])
        QT = p.sbuf([128, 4, NT], BF16, st)
        KT = p.sbuf([128, 4, 20, 128], BF16, st)
        Vs = p.sbuf([128, 20, 512], BF16, st)
        OA = p.sbuf([128, 4, NT], F32, st)
        DA = p.sbuf([128, 4, NT], F32, st)
        qbs = Rot([p.sbuf([128, 512], BF16, st) for _ in range(6)])
        tAs = Rot([p.sbuf([128, 4, 32], F32, st) for _ in range(2)])
        tBs = Rot([p.sbuf([128, 4, 32], F32, st) for _ in range(2)])
        PTs = Rot([p.sbuf([128, 512], BF16, st) for _ in range(6)])
        ps = [p.psum([128, 512], F32, st) for _ in range(8)]
        p.op("pool", lambda e: e.memset(KT[:], 0.0), [], [KT])
        p.op("pool", lambda e: e.memset(Vs[:], 0.0), [], [Vs])
        w_in = W["w_in"]
        sc = 128.0 ** -0.5
        def group_w_pieces(g):
            out = []
            for blk in range(3):
                c0 = blk * 1536 + g * 512
                for k2 in range(4):
                    eng = ("pool", "dve")[len(out) % 2] if (g == 0 and norm is None) else "pool"
                    out.append(lambda blk=blk, k2=k2, c0=c0, eng=eng: load_cast(
                        p, stg, wq, wq[:, 2 * k2:2 * k2 + 2, blk * 512:(blk + 1) * 512], w_in,
                        w_in[l, k2 * 256:(k2 + 1) * 256, c0:c0 + 512].rearrange("(kc p) n -> p kc n", p=128),
                        lambda s: s[:], eng=eng))
            return out

        def load_group_w(g):
            for f in group_w_pieces(g):
                f()

        w0_pieces = group_w_pieces(0)
        if norm is None:
            for f in w0_pieces:
                f()
        if norm is not None:
            x_d, g_buf, g_ap = norm
            of = OA[:].rearrange("p a b -> p (a b)")
            df = DA[:].rearrange("p a b -> p (a b)")
            dfb = df.bitcast(BF16)
            xs_ = Rot([Buf(of[:, k * 1024:(k + 1) * 1024], "nx%d" % k) for k in range(3)])
            junks_ = Rot([Buf(of[:, 3072 + k * 1024:4096 + k * 1024], "nj%d" % k) for k in range(2)])
            u0s_ = Rot([Buf(dfb[:, k * 1024:(k + 1) * 1024], "nu%d" % k) for k in range(2)])
            gB = Buf(dfb[:, 2048:3072].rearrange("p (a b) -> p a b", a=8), "ngB")
            gT = Buf(df[:, 2048:2056], "ngT")
            sss_ = Rot([Buf(df[:, 2100 + k:2101 + k], "nss%d" % k) for k in range(3)])
            rstds_ = Rot([Buf(df[:, 2110 + k:2111 + k], "nrs%d" % k) for k in range(3)])
            for b_ in xs_.b:
                st.bufs.append(b_)
            st.bufs.append(gT)
            p.dma(gT, gT[:], g_buf, g_ap.rearrange("(c p) -> p c", p=128), allow_slow_non_contiguous=True)
            p.op("dve", lambda e: e.tensor_copy(out=gB[:], in_=gT[:].unsqueeze(2).to_broadcast([128, 8, 128])),
                 [gT], [gB])
            npss = Rot([ps[6], ps[7]])
            pend = None
            for t in range(16):
                xb = xs_.next()
                p.dma(xb, xb[:], x_d, x_d[t * 128:(t + 1) * 128, :])
                if t >= 2 and w0_pieces:
                    w0_pieces.pop(0)()
                u0 = u0s_.next()
                norm_chain(p, xb, xb[:], sss_.next(), rstds_.next(), junks_.next(), u0)
                if pend is not None:
                    norm_T(p, pend[0], npss.next(), ident, gB, uT, pend[1])
                pend = (u0, t)
            norm_T(p, pend[0], npss.next(), ident, gB, uT, pend[1])
            for f in w0_pieces:
                f()
            p.dma(uT_d, uT_d[:], uT, uT[:])
            p.barrier()
        for g, (D, T) in enumerate(GROUPS):

            def tokslice(jt):
                r, a = jt // T, jt % T
                base = r + D * 128 * a
                return r, a, slice(base, base + 127 * D + 1, D)

            qk_tiles = {}

            def proj_mm(jt):
                r, a, tok = tokslice(jt)
                pq, pk, pv = (ps[(jt % 2) * 3 + i] for i in range(3))
                for blk, pb in enumerate((pq, pk, pv)):
                    for kc in range(8):
                        p.op("pe", lambda e: e.matmul(pb[:], lhsT=uT[:, kc, tok],
                                                      rhs=wq[:, kc, blk * 512:(blk + 1) * 512],
                                                      start=(kc == 0), stop=(kc == 7)), [uT, wq], [pb])

            def proj_evac(jt):
                r, a, tok = tokslice(jt)
                pq, pk, pv = (ps[(jt % 2) * 3 + i] for i in range(3))
                c3 = rope[:, g, jt, 0:16].unsqueeze(1).to_broadcast([128, 4, 16])
                s3 = rope[:, g, jt, 16:32].unsqueeze(1).to_broadcast([128, 4, 16])
                qk = []
                for which, pb in enumerate((pq, pk)):
                    qb = qbs.next()
                    tA, tB = tAs.next(), tBs.next()
                    pv3 = pb[:].rearrange("p (h d) -> p h d", h=4)
                    qb3 = qb[:].rearrange("p (h d) -> p h d", h=4)
                    x1, x2 = pv3[:, :, 0:16], pv3[:, :, 16:32]
                    p.op("act", lambda e: e.activation(out=qb[:], in_=pb[:], func=AF.Copy), [pb], [qb])
                    p.op("dve", lambda e: e.tensor_tensor(out=tA[:, :, 0:16], in0=x1, in1=c3, op=ALU.mult), [pb, rope], [tA])
                    p.op("dve", lambda e: e.tensor_tensor(out=tA[:, :, 16:32], in0=x2, in1=c3, op=ALU.mult), [pb, rope], [tA])
                    p.op("dve", lambda e: e.tensor_tensor(out=tB[:, :, 0:16], in0=x1, in1=s3, op=ALU.mult), [pb, rope], [tB])
                    p.op("dve", lambda e: e.tensor_tensor(out=tB[:, :, 16:32], in0=x2, in1=s3, op=ALU.mult), [pb, rope], [tB])
                    p.op("dve", lambda e: e.tensor_tensor(out=qb3[:, :, 0:16], in0=tA[:, :, 0:16],
                                                          in1=tB[:, :, 16:32], op=ALU.subtract), [tA, tB], [qb])
                    p.op("dve", lambda e: e.tensor_tensor(out=qb3[:, :, 16:32], in0=tA[:, :, 16:32],
                                                          in1=tB[:, :, 0:16], op=ALU.add), [tA, tB], [qb])
                    qk.append(qb)
                qk_tiles[jt] = qk
                if T == 1:
                    p.op("act", lambda e: e.activation(out=Vs[:, jt, :], in_=pv[:], func=AF.Copy), [pv], [Vs])
                else:
                    s0 = r * (T + 1) + a
                    p.op("act", lambda e: e.activation(out=Vs[0:64, s0, :], in_=pv[0:64, :], func=AF.Copy),
                         [pv], [Vs])
                    p.op("act", lambda e: e.activation(out=Vs[64:128, s0 + 1, :], in_=pv[64:128, :], func=AF.Copy),
                         [pv], [Vs])

            def proj_T(jt):
                r, a, tok = tokslice(jt)
                for which, qb in enumerate(qk_tiles.pop(jt)):
                    pt = ps[6 + which]
                    ptv = pt[:].bitcast(BF16)
                    for h in range(4):
                        p.op("pe", lambda e: e.transpose(ptv[:, h * 128:(h + 1) * 128],
                                                         qb[:, h * 128:(h + 1) * 128], ident[:]),
                             [qb, ident], [pt])
                    src3 = ptv[:, 0:512].rearrange("p (h d) -> p h d", h=4)
                    if which == 0:
                        p.op("dve", lambda e: e.tensor_copy(out=QT[:, :, jt * 128:(jt + 1) * 128], in_=src3),
                             [pt], [QT])
                    elif T == 1:
                        p.op("act", lambda e: e.activation(out=KT[:, :, jt, :], in_=src3, func=AF.Copy), [pt], [KT])
                    else:
                        s0k = r * (T + 1) + a
                        p.op("act", lambda e: e.activation(out=KT[:, :, s0k, 0:64], in_=src3[:, :, 0:64],
                                                           func=AF.Copy), [pt], [KT])
                        p.op("act", lambda e: e.activation(out=KT[:, :, s0k + 1, 64:128], in_=src3[:, :, 64:128],
                                                           func=AF.Copy), [pt], [KT])

            proj_mm(0)
            proj_evac(0)
            proj_mm(1)
            proj_evac(1)
            for jt in range(2, 16):
                proj_mm(jt)
                proj_T(jt - 2)
                proj_evac(jt)
            proj_T(14)
            proj_T(15)
            if g + 1 < len(GROUPS):
                load_group_w(g + 1)

            stA = {}

            def coreA(jt):
                r, a, tok = tokslice(jt)
                bs = (jt % 2) * 4
                if T == 1:
                    blocks = [(jt, jt, 4)]
                else:
                    s0 = r * (T + 1) + a
                    blocks = [(s0, s0, 1 if a == 0 else 0), (s0 + 1, s0 + 1, 3 if a == T - 1 else 2)]
                PT = []
                for bi, (kslot, slot, mi) in enumerate(blocks):
                    pS = ps[bs + bi]
                    for h in range(4):
                        p.op("pe", lambda e: e.matmul(pS[:, h * 128:(h + 1) * 128], lhsT=KT[:, h, kslot, :],
                                                      rhs=QT[:, h, jt * 128:(jt + 1) * 128],
                                                      start=True, stop=True), [KT, QT], [pS])
                    pt_ = PTs.next()
                    p.op("act", lambda e: e.activation(out=pt_[:], in_=pS[:], func=AF.Exp, scale=sc), [pS], [pt_])
                    p3 = pt_[:].rearrange("p (h q) -> p h q", h=4)
                    p.op("dve", lambda e: e.tensor_tensor(
                        out=p3, in0=p3, in1=masks[:, mi, :].unsqueeze(1).to_broadcast([128, 4, 128]),
                        op=ALU.mult), [pt_, masks], [pt_])
                    PT.append(pt_)
                stA[jt] = (blocks, PT)

            def coreB(jt):
                r, a, tok = tokslice(jt)
                bs = (jt % 2) * 4
                blocks, PT = stA.pop(jt)
                pO, pD = ps[bs + 2], ps[bs + 3]
                nb_ = len(blocks)
                for h in range(4):
                    for bi, (kslot, slot, mi) in enumerate(blocks):
                        p.op("pe", lambda e: e.matmul(pO[:, h * 128:(h + 1) * 128],
                                                      lhsT=Vs[:, slot, h * 128:(h + 1) * 128],
                                                      rhs=PT[bi][:, h * 128:(h + 1) * 128],
                                                      start=(bi == 0), stop=(bi == nb_ - 1)), [Vs, PT[bi]], [pO])
                for bi in range(nb_):
                    p.op("pe", lambda e: e.matmul(pD[:], lhsT=ones[:], rhs=PT[bi][:],
                                                  start=(bi == 0), stop=(bi == nb_ - 1)), [ones, PT[bi]], [pD])
                o3 = pO[:].rearrange("p (h q) -> p h q", h=4)
                d3 = pD[:].rearrange("p (h q) -> p h q", h=4)
                if g == 0:
                    p.op("dve", lambda e: e.tensor_copy(out=OA[:, :, tok], in_=o3), [pO], [OA])
                    p.op("act", lambda e: e.activation(out=DA[:, :, tok], in_=d3, func=AF.Copy), [pD], [DA])
                else:
                    p.op("dve", lambda e: e.tensor_tensor(out=OA[:, :, tok], in0=o3, in1=OA[:, :, tok],
                                                          op=ALU.add), [pO, OA], [OA])
                    p.op("dve", lambda e: e.tensor_tensor(out=DA[:, :, tok], in0=d3, in1=DA[:, :, tok],
                                                          op=ALU.add), [pD, DA], [DA])

            coreA(0)
            for jt in range(1, 16):
                coreA(jt)
                coreB(jt - 1)
            coreB(15)
        def alias(b, name):
            a = Buf(b.t, name)
            a.writers, a.readers = dict(b.writers), dict(b.readers)
            return a

        for h in range(4):
            DAh, QTh = alias(DA, "DAh%d" % h), alias(QT, "QTh%d" % h)
            p.op("act", lambda e: e.activation(out=DA[:, h, :], in_=DA[:, h, :], func=AF.Ln), [DAh], [DAh])
            p.op("act", lambda e: e.activation(out=DA[:, h, :], in_=DA[:, h, :], func=AF.Exp, scale=-1.0),
                 [DAh], [DAh])
            p.op("dve", lambda e: e.tensor_tensor(out=QT[:, h, :], in0=OA[:, h, :], in1=DA[:, h, :],
                                                  op=ALU.mult), [OA, DAh], [QTh])
            p.dma(oT_d, oT_d[:, h, :], QTh, QT[:, h, :])


def phase_hyproj(p, C, W, l, uT_d, hv_d, hx1_d, hx2_d, uT_sb=None):
    with Phase(p) as st:
        ident = p.sbuf([128, 128], BF16, st)
        p.dma(ident, ident[:], C["ident"], C["ident"][:])
        if uT_sb is not None:
            uT = uT_sb
        else:
            uT = p.sbuf([128, 8, NT], BF16, st)
            p.dma(uT, uT[:], uT_d, uT_d[:])
        cw = p.sbuf([128, 24, 3], F32, st)
        for j in range(3):
            p.dma(cw, cw[:, :, j], W["conv_w"], W["conv_w"][l, j].rearrange("(c p) -> p c", p=128),
                  allow_slow_non_contiguous=True)
        cb = p.sbuf([128, 24], F32, st)
        p.dma(cb, cb[:], W["conv_b"], W["conv_b"][l].rearrange("(c p) -> p c", p=128),
              allow_slow_non_contiguous=True)
        toks = [p.sbuf([128, 16, DM], BF16, st) for _ in range(2)]
        stg = Rot([p.sbuf([128, 8, 128], F32, st) for _ in range(3)])
        wcs = Rot([p.sbuf([128, 8, 128], BF16, st) for _ in range(3)])
        hzs = Rot([p.sbuf([128, NT], F32, st) for _ in range(2)])
        hzbs = Rot([p.sbuf([128, NT], BF16, st) for _ in range(3)])
        pss = Rot([p.psum([128, NT], F32, st) for _ in range(2)])
        w_in = W["w_in"]
        stt = {}

        def hmm(cc):
            c0 = 4608 + cc * 128
            wc = wcs.next()
            load_cast(p, stg, wc, wc[:], w_in, w_in[l, :, c0:c0 + 128].rearrange("(kc p) n -> p kc n", p=128),
                      lambda s: s[:])
            pb = pss.next()
            stt[cc] = (pb, wc)
            hmm_part(cc, (0, 1))

        def hmm_part(cc, nbs):
            pb, wc = stt[cc]
            for nb in nbs:
                for kc in range(8):
                    p.op("pe", lambda e: e.matmul(pb[:, nb * 512:(nb + 1) * 512], lhsT=wc[:, kc, :],
                                                  rhs=uT[:, kc, nb * 512:(nb + 1) * 512],
                                                  start=(kc == 0), stop=(kc == 7)), [wc, uT], [pb])

        def hevac(cc):
            pb = stt[cc][0]
            hz, hzb = hzs.next(), hzbs.next()
            p.op("act", lambda e: e.activation(out=hz[:], in_=pb[:], func=AF.Identity,
                                               scale=cw[:, cc, 1:2], bias=cb[:, cc:cc + 1]), [pb, cw, cb], [hz])
            p.op("dve", lambda e: e.scalar_tensor_tensor(out=hz[:, 1:NT], in0=pb[:, 0:NT - 1],
                                                         scalar=cw[:, cc, 0:1], in1=hz[:, 1:NT],
                                                         op0=ALU.mult, op1=ALU.add), [pb, cw, hz], [hz])
            p.op("dve", lambda e: e.scalar_tensor_tensor(out=hzb[:, 0:NT - 1], in0=pb[:, 1:NT],
                                                         scalar=cw[:, cc, 2:3], in1=hz[:, 0:NT - 1],
                                                         op0=ALU.mult, op1=ALU.add), [pb, cw, hz], [hzb])
            p.op("dve", lambda e: e.tensor_copy(out=hzb[:, NT - 1:NT], in_=hz[:, NT - 1:NT]), [hz], [hzb])
            stt[cc] = (pb, hzb)

        def hT(cc):
            pb, hzb = stt.pop(cc)
            if cc < 16:
                tk = toks[cc // 8]
                c8 = cc % 8
                ptv = pb[:].bitcast(BF16)
                for t in range(16):
                    p.op("pe", lambda e: e.transpose(ptv[:, t * 128:(t + 1) * 128], hzb[:, t * 128:(t + 1) * 128],
                                                     ident[:]), [hzb, ident], [pb])
                src3 = ptv[:, 0:2048].rearrange("p (t c) -> p t c", t=16)
                p.op("act", lambda e: e.activation(out=tk[:, :, c8 * 128:(c8 + 1) * 128], in_=src3,
                                                   func=AF.Copy), [pb], [tk])
                if cc == 7:
                    p.dma(hv_d, hv_d[:], toks[0], toks[0][:], q="act")
                if cc == 15:
                    p.dma(hx1_d, hx1_d[:], toks[1], toks[1][:], q="act")
            else:
                p.dma(hx2_d, hx2_d[:, cc - 16, :], hzb, hzb[:], q="act")

        hmm(0)
        hmm_part(0, (2, 3))
        hevac(0)
        for cc in range(1, 24):
            hmm(cc)
            hmm_part(cc, (2, 3))
            hT(cc - 1)
            hevac(cc)
        hT(23)


def sin_reduced(p, pre_buf, pre_ap, nrows, fr, fb, vbuf, sbuf_, out_ap, out_buf):
    p.op("dve", lambda e: e.tensor_scalar(out=vbuf[0:nrows, :], in0=pre_ap, scalar1=fr[0:nrows, 0:1],
                                          scalar2=fb[0:nrows, 0:1], op0=ALU.mult, op1=ALU.add),
         [pre_buf, fr, fb], [vbuf])
    p.op("act", lambda e: e.activation(out=sbuf_[0:nrows, :], in_=vbuf[0:nrows, :], func=AF.Sign), [vbuf], [sbuf_])
    p.op("dve", lambda e: e.scalar_tensor_tensor(out=vbuf[0:nrows, :], in0=sbuf_[0:nrows, :], scalar=-math.pi,
                                                 in1=vbuf[0:nrows, :], op0=ALU.mult, op1=ALU.add),
         [sbuf_, vbuf], [vbuf])
    p.op("act", lambda e: e.activation(out=out_ap, in_=vbuf[0:nrows, :], func=AF.Sin, scale=-1.0),
         [vbuf], [out_buf])


def phase_filters(p, C, W, l, kab_d):
    def alias(b, name):
        a = Buf(b.t, name)
        a.writers, a.readers = dict(b.writers), dict(b.readers)
        a.is_psum = b.is_psum
        return a

    with Phase(p) as st:
        SD = [p.sbuf([128, 16, DM], BF16, st) for _ in range(4)]
        h2T = p.sbuf([64, NT], BF16, st)
        w3sd = p.sbuf([64, 4096], BF16, st)
        bias = p.sbuf([1, 2, DM], F32, st)
        p.dma(bias, bias[:], W["hyena_bias"], W["hyena_bias"][l:l + 1])
        decs = Rot([p.sbuf([128, DM], F32, st) for _ in range(2)])
        tmp0 = p.sbuf([128, DM], F32, st)
        pm = p.psum([128, 4096], F32, st)
        H = [pm, Buf(pm.t, "pm_hi")]
        H[1].is_psum = True
        with Phase(p) as s2:
            zT = p.sbuf([33, NT], F32, s2)
            p.dma(zT, zT[:], C["zT"], C["zT"][:])
            w1 = p.sbuf([33, 64], F32, s2)
            p.dma(w1, w1[:], W["filt_w1"], W["filt_w1"][l])
            w2 = p.sbuf([64, 64], F32, s2)
            p.dma(w2, w2[:], W["filt_w2"], W["filt_w2"][l])
            w3f = p.sbuf([64, 2048], F32, s2)
            w3 = p.sbuf([64, 4096], BF16, s2)
            for hh in range(2):
                p.dma(w3f, w3f[:], W["filt_w3"], W["filt_w3"][l, :, hh * 2048:(hh + 1) * 2048])
                p.op("pool", lambda e: e.tensor_copy(out=w3[:, hh * 2048:(hh + 1) * 2048], in_=w3f[:]), [w3f], [w3])
            vecs = {}
            for nm in ("filt_b1", "filt_freq1", "filt_b2", "filt_freq2"):
                v = p.sbuf([64, 1], F32, s2)
                p.dma(v, v[:], W[nm], W[nm][l].rearrange("(p o) -> p o", o=1))
                vecs[nm] = v
            fb1 = p.sbuf([64, 1], F32, s2)
            fb2 = p.sbuf([64, 1], F32, s2)
            p.op("dve", lambda e: e.tensor_tensor(out=fb1[:], in0=vecs["filt_b1"][:], in1=vecs["filt_freq1"][:],
                                                  op=ALU.mult), [vecs["filt_b1"], vecs["filt_freq1"]], [fb1])
            p.op("dve", lambda e: e.tensor_tensor(out=fb2[:], in0=vecs["filt_b2"][:], in1=vecs["filt_freq2"][:],
                                                  op=ALU.mult), [vecs["filt_b2"], vecs["filt_freq2"]], [fb2])
            for o in range(2):
                f_, b_ = w3[:, (2 * o) * 1024:(2 * o + 1) * 1024], w3[:, (2 * o + 1) * 1024:(2 * o + 2) * 1024]
                p.op("dve", lambda e: e.tensor_tensor(out=w3sd[:, (2 * o) * 1024:(2 * o + 1) * 1024], in0=f_, in1=b_,
                                                      op=ALU.add), [w3], [w3sd])
                p.op("dve", lambda e: e.tensor_tensor(out=w3sd[:, (2 * o + 1) * 1024:(2 * o + 2) * 1024], in0=f_,
                                                      in1=b_, op=ALU.subtract), [w3], [w3sd])
            h1T = p.sbuf([64, NT], F32, s2)
            vbs = Rot([p.sbuf([64, 512], F32, s2) for _ in range(2)])
            sbs = Rot([p.sbuf([64, 512], F32, s2) for _ in range(2)])
            for nb in range(4):
                sl = slice(nb * 512, (nb + 1) * 512)
                pq = H[nb % 2]
                c0 = (nb % 2) * 2048
                p.op("pe", lambda e: e.matmul(pm[0:64, c0:c0 + 512], lhsT=w1[:], rhs=zT[:, sl], start=True, stop=True),
                     [w1, zT], [pq])
                sin_reduced(p, pq, pm[0:64, c0:c0 + 512], 64, vecs["filt_freq1"], fb1, vbs.next(), sbs.next(),
                            h1T[:, sl], h1T)
            for nb in range(4):
                sl = slice(nb * 512, (nb + 1) * 512)
                pq = H[nb % 2]
                c0 = (nb % 2) * 2048
                p.op("pe", lambda e: e.matmul(pm[0:64, c0:c0 + 512], lhsT=w2[:], rhs=h1T[:, sl], start=True, stop=True),
                     [w2, h1T], [pq])
                sin_reduced(p, pq, pm[0:64, c0:c0 + 512], 64, vecs["filt_freq2"], fb2, vbs.next(), sbs.next(),
                            h2T[:, sl], h2T)

        def gen(t, o, half):
            dec = decs.next()
            p.dma(dec, dec[:], C["decay"], C["decay"][:, t, :])
            hb = H[half]
            ph_ = pm[:, half * 2048:(half + 1) * 2048]
            for cbk in range(4):
                col = o * 2048 + cbk * 512
                p.op("pe", lambda e: e.matmul(ph_[:, cbk * 512:(cbk + 1) * 512], lhsT=h2T[:, t * 128:(t + 1) * 128],
                                              rhs=w3sd[:, col:col + 512], start=True, stop=True), [h2T, w3sd], [hb])
            if t == 0:
                p.op("dve", lambda e: e.tensor_tensor(out=tmp0[:], in0=ph_[:, 0:1024], in1=dec[:], op=ALU.mult),
                     [hb, dec], [tmp0])
                p.op("dve", lambda e: e.tensor_tensor(out=tmp0[0:1, :], in0=tmp0[0:1, :], in1=bias[0:1, o, :],
                                                      op=ALU.add), [tmp0, bias], [tmp0])
                p.op("dve", lambda e: e.tensor_copy(out=SD[2 * o][:, t, :], in_=tmp0[:]), [tmp0], [SD[2 * o]])
            else:
                p.op("dve", lambda e: e.tensor_tensor(out=SD[2 * o][:, t, :], in0=ph_[:, 0:1024], in1=dec[:],
                                                      op=ALU.mult), [hb, dec], [SD[2 * o]])
            p.op("dve", lambda e: e.tensor_tensor(out=SD[2 * o + 1][:, t, :], in0=ph_[:, 1024:2048], in1=dec[:],
                                                  op=ALU.mult), [hb, dec], [SD[2 * o + 1]])

        for t in range(16):
            gen(t, 0, t % 2)

        ffc = Rot([p.sbuf([128, 16, 128], BF16, st) for _ in range(2)])
        ffs = Rot([p.sbuf([128, 16, 128], BF16, st) for _ in range(2)])
        outs = Rot([p.sbuf([128, DM], F32, st) for _ in range(4)])
        K = [alias(H[0], "pmK0"), alias(H[0], "pmK1")]

        def spec_block(k, fch, F_, unit, col0):
            pk = pm[:, col0:col0 + 1024]
            for cbk in range(2):
                for tc in range(16):
                    p.op("pe", lambda e: e.matmul(pk[:, cbk * 512:(cbk + 1) * 512], lhsT=F_[:, tc, :],
                                                  rhs=SD[k][:, tc, cbk * 512:(cbk + 1) * 512],
                                                  start=(tc == 0), stop=(tc == 15)), [F_, SD[k]], [unit])
            ob = outs.next()
            if k % 2 == 0:
                p.op("act", lambda e: e.activation(out=ob[:], in_=pk, func=AF.Copy), [unit], [ob])
            else:
                p.op("dve", lambda e: e.tensor_copy(out=ob[:], in_=pk), [unit], [ob])
            p.dma(kab_d, kab_d[k, fch], ob, ob[:], q="act")

        for fch in range(16):
            fc_, fs_ = ffc.next(), ffs.next()
            p.dma(fc_, fc_[:], C["ffc"], C["ffc"][fch])
            p.dma(fs_, fs_[:], C["ffs"], C["ffs"][fch])
            gen(fch, 1, 1)
            spec_block(0, fch, fc_, K[0], 0)
            spec_block(1, fch, fs_, K[1], 1024)
        K += [alias(H[1], "pmK2"), alias(H[1], "pmK3")]
        for fch in range(16):
            fc_, fs_ = ffc.next(), ffs.next()
            p.dma(fc_, fc_[:], C["ffc"], C["ffc"][fch])
            p.dma(fs_, fs_[:], C["ffs"], C["ffs"][fch])
            b0 = (fch % 2) * 2
            spec_block(2, fch, fc_, K[b0], b0 * 1024)
            spec_block(3, fch, fs_, K[b0 + 1], (b0 + 1) * 1024)


def phase_fft(p, C, kab_d, hv_d, hx1_d, hx2_d, z2T_d):
    with Phase(p) as st:
        xt = p.sbuf([128, 16, DM], BF16, st)
        p.dma(xt, xt[:], hv_d, hv_d[:])
        Y = p.sbuf([128, 32, DM], BF16, st)
        hx = p.sbuf([128, 16 * DM], BF16, st)
        ffc = Rot([p.sbuf([128, 16, 128], BF16, st) for _ in range(2)])
        ffs = Rot([p.sbuf([128, 16, 128], BF16, st) for _ in range(2)])
        kas = Rot([p.sbuf([128, DM], F32, st) for _ in range(2)])
        kbs = Rot([p.sbuf([128, DM], F32, st) for _ in range(2)])
        tmp = [p.sbuf([128, DM], F32, st) for _ in range(4)]
        fis = Rot([p.sbuf([128, 8, 512], BF16, st) for _ in range(2)])
        psA = [p.psum([128, DM], F32, st) for _ in range(2)]
        psB = [p.psum([128, DM], F32, st) for _ in range(2)]

        def forward(conv):
            for fch in range(16):
                if conv == 0 and fch == 2:
                    p.dma(hx, hx[:], hx1_d, hx1_d[:].rearrange("p a b -> p (a b)"))
                if conv == 1 and fch == 2:
                    p.dma(hx, hx[:], hx2_d, hx2_d[:].rearrange("p a b -> p (a b)"))
                fc_, fs_ = ffc.next(), ffs.next()
                p.dma(fc_, fc_[:], C["ffc"], C["ffc"][fch])
                p.dma(fs_, fs_[:], C["ffs"], C["ffs"][fch])
                ka, kb = kas.next(), kbs.next()
                p.dma(ka, ka[:], kab_d, kab_d[2 * conv, fch])
                p.dma(kb, kb[:], kab_d, kab_d[2 * conv + 1, fch])
                pa, pb = psA[fch % 2], psB[fch % 2]
                for F_, pp in ((fc_, pa), (fs_, pb)):
                    for cbk in range(2):
                        for tc in range(16):
                            p.op("pe", lambda e: e.matmul(pp[:, cbk * 512:(cbk + 1) * 512], lhsT=F_[:, tc, :],
                                                          rhs=xt[:, tc, cbk * 512:(cbk + 1) * 512],
                                                          start=(tc == 0), stop=(tc == 15)), [F_, xt], [pp])
                p.op("dve", lambda e: e.tensor_tensor(out=tmp[0][:], in0=pa[:], in1=ka[:], op=ALU.mult), [pa, ka], [tmp[0]])
                p.op("dve", lambda e: e.tensor_tensor(out=tmp[1][:], in0=pb[:], in1=kb[:], op=ALU.mult), [pb, kb], [tmp[1]])
                p.op("pool", lambda e: e.tensor_tensor(out=Y[:, fch, :], in0=tmp[0][:], in1=tmp[1][:], op=ALU.subtract),
                     [tmp[0], tmp[1]], [Y])
                p.op("dve", lambda e: e.tensor_tensor(out=tmp[2][:], in0=pa[:], in1=kb[:], op=ALU.mult), [pa, kb], [tmp[2]])
                p.op("dve", lambda e: e.tensor_tensor(out=tmp[3][:], in0=pb[:], in1=ka[:], op=ALU.mult), [pb, ka], [tmp[3]])
                p.op("pool", lambda e: e.tensor_tensor(out=Y[:, 16 + fch, :], in0=tmp[2][:], in1=tmp[3][:], op=ALU.add),
                     [tmp[2], tmp[3]], [Y])

        forward(0)
        banks = [psA[0], psB[0], psA[1], psB[1]]
        hx3 = hx[:].rearrange("p (a b) -> p a b", a=16)
        for nb in range(4):
            for fg in range(4):
                fi = fis.next()
                p.dma(fi, fi[:], C["finv"], C["finv"][nb, :, fg * 8:(fg + 1) * 8, :])
                order = ([(f8, tl) for f8 in range(8) for tl in range(4)] if fg < 3 else
                         [(f8, tl) for tl in range(4) for f8 in range(8)])
                for f8, tl in order:
                    fc = fg * 8 + f8
                    for cbk in range(2):
                        p.op("pe", lambda e: e.matmul(banks[tl][:, cbk * 512:(cbk + 1) * 512],
                                                      lhsT=fi[:, f8, tl * 128:(tl + 1) * 128],
                                                      rhs=Y[:, fc, cbk * 512:(cbk + 1) * 512],
                                                      start=(fc == 0), stop=(fc == 31)), [fi, Y], [banks[tl]])
            for tl in range(4):
                t = nb * 4 + tl
                p.op("dve", lambda e: e.tensor_tensor(out=xt[:, t, :], in0=banks[tl][:], in1=hx3[:, t, :],
                                                      op=ALU.mult), [banks[tl], hx], [xt])
        forward(1)
        hc3 = hx[:].rearrange("p (a b) -> p a b", a=8)
        for nb in range(4):
            for fg in range(4):
                fi = fis.next()
                p.dma(fi, fi[:], C["finv"], C["finv"][nb, :, fg * 8:(fg + 1) * 8, :])
                order = ([(f8, cc) for f8 in range(8) for cc in range(8)] if fg < 3 else
                         [(f8, cc) for cc in range(8) for f8 in range(8)])
                for f8, cc in order:
                    fc = fg * 8 + f8
                    bk = banks[cc // 2]
                    p.op("pe", lambda e: e.matmul(bk[:, (cc % 2) * 512:(cc % 2 + 1) * 512],
                                                  lhsT=Y[:, fc, cc * 128:(cc + 1) * 128], rhs=fi[:, f8, :],
                                                  start=(fc == 0), stop=(fc == 31)), [Y, fi], [bk])
            for cc in range(8):
                bk = banks[cc // 2]
                p.op("dve", lambda e: e.tensor_tensor(out=hc3[:, cc, nb * 512:(nb + 1) * 512],
                                                      in0=bk[:, (cc % 2) * 512:(cc % 2 + 1) * 512],
                                                      in1=hc3[:, cc, nb * 512:(nb + 1) * 512], op=ALU.mult),
                     [bk, hx], [hx])
            p.dma(z2T_d, z2T_d[:, :, nb * 512:(nb + 1) * 512], hx, hc3[:, :, nb * 512:(nb + 1) * 512], q="act")


def post_norm_residual(p, pacc, gpb, x_src_d, t, xs, junk, ss, rstd, tmpf, xnew):
    xb = xs.next()
    p.dma(xb, xb[:], x_src_d, x_src_d[t * 128:(t + 1) * 128, :])
    p.op("act", lambda e: e.activation(out=junk[:], in_=pacc[:], func=AF.Square, accum_out=ss[:]), [pacc], [junk, ss])
    rstd_from_ss(p, ss, rstd)
    p.op("dve", lambda e: e.scalar_tensor_tensor(out=tmpf[:], in0=pacc[:], scalar=rstd[:, 0:1], in1=gpb[:],
                                                 op0=ALU.mult, op1=ALU.mult), [pacc, rstd, gpb], [tmpf])
    p.op("pool", lambda e: e.tensor_tensor(out=xnew[:], in0=tmpf[:], in1=xb[:], op=ALU.add), [tmpf, xb], [xnew])


def phase_merge(p, C, W, l, uT_d, oT_d, z2T_d, x_src_d, xa_d, u2T_d):
    with Phase(p) as st:
        mT = p.sbuf([128, 8, NT], BF16, st)
        wo = p.sbuf([128, 8, DM], BF16, st)
        with Phase(p) as s2:
            stg = Rot([p.sbuf([128, 28, 128], F32, s2) for _ in range(2)])
            wbs = Rot([p.sbuf([128, 28, 128], BF16, s2) for _ in range(2)])

            def load_w(mc):
                cs = slice(mc * 128, (mc + 1) * 128)
                s_ = stg.next()
                wb_ = wbs.next()
                p.dma(s_, s_[:, 0:4, :], W["w_o_attn"], W["w_o_attn"][l, :, cs].rearrange("(k p) n -> p k n", p=128))
                p.dma(s_, s_[:, 4:12, :], W["w_o_hyena"], W["w_o_hyena"][l, :, cs].rearrange("(k p) n -> p k n", p=128))
                p.dma(s_, s_[:, 12:20, :], W["w_gate"], W["w_gate"][l, :, cs].rearrange("(k p) n -> p k n", p=128))
                p.dma(s_, s_[:, 20:28, :], W["w_gate"],
                      W["w_gate"][l, :, 1024 + mc * 128:1024 + (mc + 1) * 128].rearrange("(k p) n -> p k n", p=128))
                p.op("pool", lambda e: e.tensor_copy(out=wb_[:], in_=s_[:]), [s_], [wb_])
                return wb_

            wb_next = load_w(0)
            uTs = [p.sbuf([128, 8, 512], BF16, s2) for _ in range(4)]
            oTs = [p.sbuf([128, 4, 512], BF16, s2) for _ in range(4)]
            zTs = [p.sbuf([128, 8, 512], BF16, s2) for _ in range(4)]
            for nb in range(4):
                ns = slice(nb * 512, (nb + 1) * 512)
                p.dma(oTs[nb], oTs[nb][:], oT_d, oT_d[:, :, ns])
                p.dma(zTs[nb], zTs[nb][:], z2T_d, z2T_d[:, :, ns])
                p.dma(uTs[nb], uTs[nb][:], uT_d, uT_d[:, :, ns])
            bg = p.sbuf([128, 16], F32, s2)
            p.dma(bg, bg[:], W["b_gate"], W["b_gate"][l].rearrange("(c p) -> p c", p=128),
                  allow_slow_non_contiguous=True)
            sg = [Rot([p.sbuf([128, 512], F32, s2) for _ in range(2)]) for _ in range(2)]
            mm = [Rot([p.sbuf([128, 512], F32, s2) for _ in range(2)]) for _ in range(2)]
            pss = [p.psum([128, 512], F32, s2) for _ in range(8)]
            it = 0
            for mc in range(8):
                wb = wb_next
                if mc + 1 < 8:
                    wb_next = load_w(mc + 1)
                for nb in range(4):
                    ns = slice(nb * 512, (nb + 1) * 512)
                    pa, ph, pg0, pg1 = (pss[(it % 2) * 4 + i] for i in range(4))
                    it += 1
                    for k in range(4):
                        p.op("pe", lambda e: e.matmul(pa[:], lhsT=wb[:, k, :], rhs=oTs[nb][:, k, :], start=(k == 0),
                                                      stop=(k == 3)), [wb, oTs[nb]], [pa])
                    for k in range(8):
                        p.op("pe", lambda e: e.matmul(ph[:], lhsT=wb[:, 4 + k, :], rhs=zTs[nb][:, k, :], start=(k == 0),
                                                      stop=(k == 7)), [wb, zTs[nb]], [ph])
                    for k in range(8):
                        p.op("pe", lambda e: e.matmul(pg0[:], lhsT=wb[:, 12 + k, :], rhs=uTs[nb][:, k, :], start=(k == 0),
                                                      stop=(k == 7)), [wb, uTs[nb]], [pg0])
                    for k in range(8):
                        p.op("pe", lambda e: e.matmul(pg1[:], lhsT=wb[:, 20 + k, :], rhs=uTs[nb][:, k, :], start=(k == 0),
                                                      stop=(k == 7)), [wb, uTs[nb]], [pg1])
                    s0, s1 = sg[0].next(), sg[1].next()
                    m0, m1 = mm[0].next(), mm[1].next()
                    p.op("act", lambda e: e.activation(out=s0[:], in_=pg0[:], func=AF.Sigmoid, bias=bg[:, mc:mc + 1]),
                         [pg0, bg], [s0])
                    p.op("act", lambda e: e.activation(out=s1[:], in_=pg1[:], func=AF.Sigmoid,
                                                       bias=bg[:, 8 + mc:9 + mc]), [pg1, bg], [s1])
                    p.op("dve", lambda e: e.tensor_tensor(out=m0[:], in0=pa[:], in1=s0[:], op=ALU.mult), [pa, s0], [m0])
                    p.op("dve", lambda e: e.tensor_tensor(out=m1[:], in0=ph[:], in1=s1[:], op=ALU.mult), [ph, s1], [m1])
                    p.op("dve", lambda e: e.tensor_tensor(out=mT[:, mc, ns], in0=m0[:], in1=m1[:], op=ALU.add),
                         [m0, m1], [mT])
                if mc >= 4:
                    i = mc - 4
                    load_cast(p, stg, wo, wo[:, 2 * i:2 * i + 2, :], W["w_out"],
                              W["w_out"][l, i * 256:(i + 1) * 256, :].rearrange("(k p) n -> p k n", p=128),
                              lambda s_: s_[:].rearrange("p a b -> p (a b)")[:, 0:2048].rearrange("p (k n) -> p k n", k=2))
        ident = p.sbuf([128, 128], BF16, st)
        p.dma(ident, ident[:], C["ident"], C["ident"][:])
        gpb = p.sbuf([128, DM], F32, st)
        p.dma(gpb, gpb[:], W["norm_mix_post"], W["norm_mix_post"][l:l + 1, :].partition_broadcast(128))
        gB = load_gB(p, st, W["norm_ffn_pre"], W["norm_ffn_pre"][l])
        u2T = p.sbuf([128, 8, NT], BF16, st)
        xs = Rot([p.sbuf([128, DM], F32, st) for _ in range(2)])
        xns = Rot([p.sbuf([128, DM], F32, st) for _ in range(3)])
        u0s = Rot([p.sbuf([128, DM], BF16, st) for _ in range(4)])
        junks = Rot([p.sbuf([128, DM], BF16, st) for _ in range(2)])
        tmpfs = Rot([p.sbuf([128, DM], F32, st) for _ in range(2)])
        sss = Rot([p.sbuf([128, 1], F32, st) for _ in range(6)])
        rstds = Rot([p.sbuf([128, 1], F32, st) for _ in range(6)])
        pacc = Rot([p.psum([128, DM], F32, st) for _ in range(3)])
        pts = Rot([p.psum([128, 512], F32, st) for _ in range(2)])
        u0t = {}

        def omm(t):
            pa = pacc.next()
            for cbk in range(2):
                for mc in range(8):
                    p.op("pe", lambda e: e.matmul(pa[:, cbk * 512:(cbk + 1) * 512], lhsT=mT[:, mc, t * 128:(t + 1) * 128],
                                                  rhs=wo[:, mc, cbk * 512:(cbk + 1) * 512], start=(mc == 0),
                                                  stop=(mc == 7)), [mT, wo], [pa])
            u0t[t] = pa

        def ochain(t):
            pa = u0t[t]
            xn = xns.next()
            post_norm_residual(p, pa, gpb, x_src_d, t, xs, junks.next(), sss.next(), rstds.next(), tmpfs.next(), xn)
            p.dma(xa_d, xa_d[t * 128:(t + 1) * 128, :], xn, xn[:])
            u0 = u0s.next()
            norm_chain(p, xn, xn[:], sss.next(), rstds.next(), junks.next(), u0)
            u0t[t] = u0

        def oT(t):
            norm_T(p, u0t.pop(t), pts.next(), ident, gB, u2T, t)
            if t % 4 == 3:
                nb = t // 4
                p.dma(u2T_d, u2T_d[:, :, nb * 512:(nb + 1) * 512], u2T, u2T[:, :, nb * 512:(nb + 1) * 512], q="act")

        omm(0)
        ochain(0)
        omm(1)
        ochain(1)
        for t in range(2, 16):
            omm(t)
            oT(t - 2)
            ochain(t)
        oT(14)
        oT(15)


def phase_ffn(p, C, W, l, u2T_d, xa_d, xout_d):
    with Phase(p) as st:
      hT = p.sbuf([128, 22, NT], BF16, st)
      wd = p.sbuf([128, 22, DM], BF16, st)
      with Phase(p) as s1:
        u2Ts = [p.sbuf([128, 8, 512], BF16, s1) for _ in range(4)]
        stg = Rot([p.sbuf([128, 16, 128], F32, s1) for _ in range(2)])
        wbs = Rot([p.sbuf([128, 16, 128], BF16, s1) for _ in range(2)])
        sas = Rot([p.sbuf([128, 512], F32, s1) for _ in range(3)])
        pss = [p.psum([128, 512], F32, s1) for _ in range(4)]
        wgu = W["w_gate_up"]
        it = 0

        def load_gu(fc):
            s_ = stg.next()
            wb_ = wbs.next()
            p.dma(s_, s_[:, 0:8, :], wgu, wgu[l, :, fc * 128:(fc + 1) * 128].rearrange("(k p) n -> p k n", p=128))
            p.dma(s_, s_[:, 8:16, :], wgu,
                  wgu[l, :, DFF + fc * 128:DFF + (fc + 1) * 128].rearrange("(k p) n -> p k n", p=128))
            p.op("pool", lambda e: e.tensor_copy(out=wb_[:], in_=s_[:]), [s_], [wb_])
            return wb_

        wb_next = load_gu(0)
        for nb in range(4):
            p.dma(u2Ts[nb], u2Ts[nb][:], u2T_d, u2T_d[:, :, nb * 512:(nb + 1) * 512])
        for fc in range(22):
            wb = wb_next
            if fc + 1 < 22:
                wb_next = load_gu(fc + 1)
            for nb in range(4):
                ns = slice(nb * 512, (nb + 1) * 512)
                pa, pb = pss[(it % 2) * 2], pss[(it % 2) * 2 + 1]
                it += 1
                for k in range(8):
                    p.op("pe", lambda e: e.matmul(pa[:], lhsT=wb[:, k, :], rhs=u2Ts[nb][:, k, :], start=(k == 0),
                                                  stop=(k == 7)), [wb, u2Ts[nb]], [pa])
                for k in range(8):
                    p.op("pe", lambda e: e.matmul(pb[:], lhsT=wb[:, 8 + k, :], rhs=u2Ts[nb][:, k, :], start=(k == 0),
                                                  stop=(k == 7)), [wb, u2Ts[nb]], [pb])
                sa = sas.next()
                p.op("act", lambda e: e.activation(out=sa[:], in_=pa[:], func=AF.Silu), [pa], [sa])
                p.op("dve", lambda e: e.tensor_tensor(out=hT[:, fc, ns], in0=pb[:], in1=sa[:], op=ALU.mult),
                     [pb, sa], [hT])
            if fc % 2 == 1:
                i = fc // 2
                load_cast(p, stg, wd, wd[:, 2 * i:2 * i + 2, :], W["w_down"],
                          W["w_down"][l, i * 256:(i + 1) * 256, :].rearrange("(k p) n -> p k n", p=128),
                          lambda s_: s_[:].rearrange("p a b -> p (a b)").rearrange("p (k n) -> p k n", k=2))
      if True:
        with Phase(p) as s2:
            gpb = p.sbuf([128, DM], F32, s2)
            p.dma(gpb, gpb[:], W["norm_ffn_post"], W["norm_ffn_post"][l:l + 1, :].partition_broadcast(128))
            xs = Rot([p.sbuf([128, DM], F32, s2) for _ in range(2)])
            xns = Rot([p.sbuf([128, DM], F32, s2) for _ in range(2)])
            junks = Rot([p.sbuf([128, DM], BF16, s2) for _ in range(2)])
            tmpfs = Rot([p.sbuf([128, DM], F32, s2) for _ in range(2)])
            sss = Rot([p.sbuf([128, 1], F32, s2) for _ in range(3)])
            rstds = Rot([p.sbuf([128, 1], F32, s2) for _ in range(3)])
            pacc = Rot([p.psum([128, DM], F32, s2) for _ in range(2)])
            for t in range(16):
                pa = pacc.next()
                for cbk in range(2):
                    for fc in range(22):
                        p.op("pe", lambda e: e.matmul(pa[:, cbk * 512:(cbk + 1) * 512],
                                                      lhsT=hT[:, fc, t * 128:(t + 1) * 128],
                                                      rhs=wd[:, fc, cbk * 512:(cbk + 1) * 512], start=(fc == 0),
                                                      stop=(fc == 21)), [hT, wd], [pa])
                xn = xns.next()
                post_norm_residual(p, pa, gpb, xa_d, t, xs, junks.next(), sss.next(), rstds.next(), tmpfs.next(), xn)
                p.dma(xout_d, xout_d[t * 128:(t + 1) * 128, :], xn, xn[:])


_OUTER = [None, None]


def _outer_open(p):
    ph = Phase(p)
    ph.__enter__()
    _OUTER[0] = ph
    _OUTER[1] = p.sbuf([128, 8, NT], BF16, ph)


def _outer_close():
    _OUTER[0].__exit__(None, None, None)
    _OUTER[0] = None
    _OUTER[1] = None


def build(n_layers=2, stop_after=None, dbg=False, needed="auto"):
    if needed == "auto":
        p1 = build(n_layers, stop_after, dbg, needed=None)
        needed = {e: sorted(v) for e, v in p1.rec.items()}
        return build(n_layers, stop_after, dbg, needed=needed).nc
    p = Prog(needed)
    W = {k: p.dram(k, s, F32, kind="ExternalInput") for k, s in IN_SHAPES.items()}
    C = {k: p.dram("c_" + k, s, dt, kind="ExternalInput") for k, (s, dt) in CONST_SHAPES.items()}
    out = p.dram("out", [NT, DM], F32, kind="ExternalOutput")
    sk = "ExternalOutput" if dbg else "Internal"
    uT_d = p.dram("uT_d", [128, 8, NT], BF16, kind=sk)
    oT_d = p.dram("oT_d", [128, 4, NT], BF16, kind=sk)
    hv_d = p.dram("hv_d", [128, 16, DM], BF16, kind=sk)
    hx1_d = p.dram("hx1_d", [128, 16, DM], BF16, kind=sk)
    hx2_d = p.dram("hx2_d", [128, 8, NT], BF16, kind=sk)
    kab_d = p.dram("kab_d", [4, 16, 128, DM], F32, kind=sk)
    z2T_d = p.dram("z2T_d", [128, 8, NT], BF16, kind=sk)
    xa_d = p.dram("xa_d", [NT, DM], F32, kind=sk)
    xb_d = p.dram("xb_d", [NT, DM], F32, kind=sk)
    u2T_d = p.dram("u2T_d", [128, 8, NT], BF16, kind=sk)
    steps = []
    for l in range(n_layers):
        x_src = W["x"] if l == 0 else xb_d
        x_dst = out if l == n_layers - 1 else xb_d
        steps += [
            ("norm", lambda: _outer_open(p)),
            ("attn", lambda l=l, x_src=x_src: phase_attn(p, C, W, l, uT_d, oT_d,
                                                         norm=(x_src, W["norm_mix_pre"], W["norm_mix_pre"][l]),
                                                         uT_sb=_OUTER[1])),
            ("hyproj", lambda l=l: (phase_hyproj(p, C, W, l, uT_d, hv_d, hx1_d, hx2_d, uT_sb=_OUTER[1]),
                                    _outer_close())),
            ("filters", lambda l=l: phase_filters(p, C, W, l, kab_d)),
            ("fft", lambda l=l: phase_fft(p, C, kab_d, hv_d, hx1_d, hx2_d, z2T_d)),
            ("merge", lambda l=l, x_src=x_src: phase_merge(p, C, W, l, uT_d, oT_d, z2T_d, x_src, xa_d, u2T_d)),
            ("ffn", lambda l=l, x_dst=x_dst: phase_ffn(p, C, W, l, u2T_d, xa_d, x_dst)),
        ]
    for i, (name, fn) in enumerate(steps):
        fn()
        if stop_after is not None and i == stop_after:
            break
    if _OUTER[0] is not None:
        _OUTER[0].__exit__(None, None, None)
        _OUTER[0] = None
    p.finish()
    return p


_CONSTS = None


def kernel(**inputs):
    global _CONSTS
    if _CONSTS is None:
        _CONSTS = make_consts()
    nc = build()
    base = {k: np.ascontiguousarray(np.asarray(v, dtype=np.float32)) for k, v in inputs.items() if k != "x"}
    for k, v in _CONSTS.items():
        base["c_" + k] = v
    x = np.asarray(inputs["x"], dtype=np.float32)
    in_maps = []
    for b in range(8):
        m = dict(base)
        m["x"] = np.ascontiguousarray(x[b])
        in_maps.append(m)
    res = run_bass_kernel_spmd(nc, in_maps, core_ids=list(range(8)))
    return np.stack([np.asarray(r["out"], dtype=np.float32) for r in res.results], axis=0)
```

```python
import math
from contextlib import ExitStack

import numpy as np
import ml_dtypes
import concourse.bass as bass
import concourse.mybir as mybir
from concourse.bass_utils import run_bass_kernel_spmd

F32 = mybir.dt.float32
BF16 = mybir.dt.bfloat16
AF = mybir.ActivationFunctionType
ALU = mybir.AluOpType

NT = 2048
DM = 1024
DFF = 2816
EPS = 1e-6
GROUPS = ((1, 16), (4, 4), (16, 1))


class Buf:
    def __init__(self, t, name):
        self.t = t
        self.name = name
        self.writers = {}
        self.readers = {}
        self.dsem = None
        self.dcount = 0
        self.is_psum = False

    def __getitem__(self, idx):
        return self.t[idx]


class Rot:
    def __init__(self, bufs):
        self.b = list(bufs)
        self.i = 0

    def next(self):
        b = self.b[self.i % len(self.b)]
        self.i += 1
        return b


class Prog:
    def __init__(self, needed=None):
        self.needed = needed
        self.needed_set = {e: set(v) for e, v in needed.items()} if needed else None
        self.rec = {}
        self.nc = bass.Bass("TRN2", target_bir_lowering=False)
        self.es = ExitStack()
        nc = self.nc
        self.eng = {"pe": nc.tensor, "act": nc.scalar, "dve": nc.vector, "pool": nc.gpsimd, "sp": nc.sync}
        self.sem = {}
        self.cnt = {}
        for e in ("pe", "act", "dve", "pool"):
            self.sem[e] = self.es.enter_context(nc.semaphore("s_" + e))
            self.cnt[e] = 0
        self.seen = {e: {} for e in self.eng}
        self.semeng = {id(self.sem[e]): e for e in self.sem}
        self.dsems = {}
        self.dsem_pool = []
        self.nbuf = 0

    def sbuf(self, shape, dtype, st=None, name=None):
        self.nbuf += 1
        name = name or f"sb{self.nbuf}"
        t = (st.st if st else self.es).enter_context(self.nc.sbuf_tensor(name, list(shape), dtype))
        b = Buf(t, name)
        if st:
            st.bufs.append(b)
        return b

    def psum(self, shape, dtype=F32, st=None, name=None):
        self.nbuf += 1
        name = name or f"ps{self.nbuf}"
        t = (st.st if st else self.es).enter_context(self.nc.psum_tensor(name, list(shape), dtype))
        b = Buf(t, name)
        b.is_psum = True
        if st:
            st.bufs.append(b)
        return b

    def dram(self, name, shape, dtype, kind="Internal"):
        t = self.nc.dram_tensor(name, list(shape), dtype, kind=kind).ap()
        return Buf(t, name)

    def _dsem(self, buf):
        if buf.dsem is None:
            if self.dsem_pool:
                buf.dsem, buf.dcount = self.dsem_pool.pop()
            else:
                buf.dsem = self.es.enter_context(self.nc.semaphore("d_" + buf.name))
        return buf.dsem

    def _wait(self, e, sem, val):
        k = id(sem)
        if self.seen[e].get(k, 0) >= val:
            return
        self.seen[e][k] = val
        se = self.semeng.get(k)
        if se is not None:
            self.rec.setdefault(se, set()).add(val)
            if self.needed is not None:
                import bisect
                val = bisect.bisect_right(self.needed[se], val)
        self.eng[e].wait_ge(sem, val)

    def _deps(self, e, reads, writes):
        need = {}
        for b in reads:
            for k, (s, v) in b.writers.items():
                if need.get(k, (None, 0))[1] < v:
                    need[k] = (s, v)
            if b.is_psum:
                own = id(self.sem[e]) if e in self.sem else None
                for k, (s, v) in b.readers.items():
                    if k != own and need.get(k, (None, 0))[1] < v:
                        need[k] = (s, v)
        for b in writes:
            for d in (b.writers, b.readers):
                for k, (s, v) in d.items():
                    if need.get(k, (None, 0))[1] < v:
                        need[k] = (s, v)
        for k, (s, v) in need.items():
            if e == "pe" and s is self.sem["pe"]:
                continue
            self._wait(e, s, v)

    def _record(self, ev, reads, writes):
        s, v = ev
        k = id(s)
        for b in writes:
            b.readers = {}
            b.writers[k] = (s, v)
        for b in reads:
            if b in writes:
                continue
            b.readers[k] = (s, v)

    def op(self, e, fn, reads=(), writes=()):
        reads = [b for b in reads if b is not None]
        writes = [b for b in writes if b is not None]
        self._deps(e, reads, writes)
        ins = fn(self.eng[e])
        self.cnt[e] += 1
        if self.needed_set is None or self.cnt[e] in self.needed_set.get(e, ()):
            ins.then_inc(self.sem[e], 1)
        self._record((self.sem[e], self.cnt[e]), reads, writes)
        return ins

    def dma(self, dst_buf, dst_ap, src_buf, src_ap, q="sp", **kw):
        self._deps(q, [src_buf], [dst_buf])
        sem = self._dsem(dst_buf)
        ins = self.eng[q].dma_start(out=dst_ap, in_=src_ap, **kw)
        dst_buf.dcount += 16
        ins.then_inc(sem, 16)
        self.dsems[id(sem)] = (sem, dst_buf.dcount)
        self._record((sem, dst_buf.dcount), [src_buf], [dst_buf])
        return ins

    def barrier(self):
        for e in self.eng:
            for e2 in self.sem:
                if e2 != e and self.cnt[e2] > 0:
                    self._wait(e, self.sem[e2], self.cnt[e2])
            for sm, v in self.dsems.values():
                self._wait(e, sm, v)

    def finish(self):
        self.barrier()
        self.es.close()
        return self.nc


class Phase:
    def __init__(self, p):
        self.p = p
        self.st = ExitStack()
        self.bufs = []

    def __enter__(self):
        self.st.__enter__()
        return self

    def __exit__(self, *a):
        self.p.barrier()
        for b in self.bufs:
            if b.dsem is not None:
                self.p.dsem_pool.append((b.dsem, b.dcount))
                b.dsem = None
        return self.st.__exit__(*a)


def _bf(a):
    return np.ascontiguousarray(a.astype(ml_dtypes.bfloat16))


def make_consts():
    c = {}
    c["ident"] = _bf(np.eye(128, dtype=np.float32))
    c["ones"] = _bf(np.ones((128, 128), dtype=np.float32))
    inv = (500000.0 ** (-np.arange(0, 32, 2, dtype=np.float32) / 32)).astype(np.float32)
    pos = np.arange(NT, dtype=np.float32)
    ang = pos[:, None] * inv[None, :]
    cs, sn = np.cos(ang).astype(np.float32), np.sin(ang).astype(np.float32)
    rope = np.zeros((3, 128, 16, 32), np.float32)
    for g, (D, T) in enumerate(GROUPS):
        for jt in range(16):
            r, a = jt // T, jt % T
            tok = r + D * (128 * a + np.arange(128))
            rope[g, :, jt, 0:16] = cs[tok]
            rope[g, :, jt, 16:32] = sn[tok]
    c["rope"] = rope
    pp = np.arange(128)[:, None]
    qq = np.arange(128)[None, :]
    lo = pp < 64
    A = np.where(lo, (qq - pp) <= 64, (pp - qq) >= 64)
    B = np.where(lo, (qq - pp) >= 64, (pp - qq) <= 64)
    Af = A & lo
    Bl = B & (~lo)
    C = np.abs(pp - qq) <= 64
    c["masks"] = _bf(np.stack([A, Af, B, Bl, C], axis=1).astype(np.float32))
    L = NT
    t = np.linspace(0.0, 1.0, L, dtype=np.float32)[:, None]
    w = (2.0 * math.pi * np.arange(L, dtype=np.float32)[:, None] / L).astype(np.float32)
    f = np.linspace(1e-4, 15, 16, dtype=np.float32)[None, :]
    z = np.concatenate([t, np.cos(f * w), -np.sin(f * w)], axis=-1).astype(np.float32)
    c["zT"] = np.ascontiguousarray(z.T)
    deltas = np.linspace(math.log(1e-2) / 1.5, math.log(1e-2) / 0.3, 1024, dtype=np.float32)
    dec = np.exp(-t * np.abs(deltas)[None, :]).astype(np.float32)
    c["decay"] = np.ascontiguousarray(dec.reshape(16, 128, 1024).transpose(1, 0, 2))
    n = np.arange(2048, dtype=np.float64)
    ph = 2.0 * np.pi * (n[:, None] * (n[None, :] + 0.5)) / 4096.0
    Fc, Fs = np.cos(ph), np.sin(ph)
    def fwd(M):
        return _bf(M.reshape(16, 128, 16, 128).transpose(2, 1, 0, 3).astype(np.float32))
    c["ffc"] = fwd(Fc)
    c["ffs"] = fwd(Fs)
    def inv_(M):
        MT = (M.T / 2048.0).reshape(16, 128, 4, 512)
        return MT.transpose(2, 1, 0, 3)
    c["finv"] = _bf(np.concatenate([inv_(Fc), inv_(Fs)], axis=2).astype(np.float32))
    return c


CONST_SHAPES = {
    "ident": ([128, 128], BF16), "ones": ([128, 128], BF16), "rope": ([3, 128, 16, 32], F32),
    "masks": ([128, 5, 128], BF16), "zT": ([33, 2048], F32), "decay": ([128, 16, 1024], F32),
    "ffc": ([16, 128, 16, 128], BF16), "ffs": ([16, 128, 16, 128], BF16),
    "finv": ([4, 128, 32, 512], BF16),
}

IN_SHAPES = {
    "x": [NT, DM], "norm_mix_pre": [2, DM], "norm_mix_post": [2, DM], "norm_ffn_pre": [2, DM],
    "norm_ffn_post": [2, DM], "w_in": [2, DM, 7680], "conv_w": [2, 3, 3072], "conv_b": [2, 3072],
    "filt_w1": [2, 33, 64], "filt_b1": [2, 64], "filt_freq1": [2, 64], "filt_w2": [2, 64, 64],
    "filt_b2": [2, 64], "filt_freq2": [2, 64], "filt_w3": [2, 64, 4096], "hyena_bias": [2, 2, 1024],
    "w_o_attn": [2, 512, DM], "w_o_hyena": [2, DM, DM], "w_gate": [2, DM, 2048], "b_gate": [2, 2048],
    "w_out": [2, DM, DM], "w_gate_up": [2, DM, 2 * DFF], "w_down": [2, DFF, DM],
}


def load_cast(p, stg, dst_buf, dst_ap, src_buf, src_ap, view, eng="pool"):
    s = stg.next()
    sv = view(s)
    p.dma(s, sv, src_buf, src_ap)
    p.op(eng, lambda e: e.tensor_copy(out=dst_ap, in_=sv), [s], [dst_buf])


def rstd_from_ss(p, ss, rstd):
    p.op("dve", lambda e: e.tensor_scalar(out=rstd[:], in0=ss[:], scalar1=1.0 / DM, scalar2=EPS,
                                          op0=ALU.mult, op1=ALU.add), [ss], [rstd])
    p.op("act", lambda e: e.activation(out=rstd[:], in_=rstd[:], func=AF.Sqrt), [rstd], [rstd])
    p.op("dve", lambda e: e.reciprocal(out=rstd[:], in_=rstd[:]), [rstd], [rstd])


def norm_chain(p, src, src_ap, ss, rstd, junk, u0):
    p.op("act", lambda e: e.activation(out=junk[:], in_=src_ap, func=AF.Square, accum_out=ss[:]),
         [src], [junk, ss])
    rstd_from_ss(p, ss, rstd)
    p.op("act", lambda e: e.activation(out=u0[:], in_=src_ap, func=AF.Copy, scale=rstd[:, 0:1]),
         [src, rstd], [u0])


def norm_T(p, u0, pst, ident, gB, uT, t):
    ptv = pst[:].bitcast(BF16)
    for kc in range(8):
        p.op("pe", lambda e: e.transpose(ptv[:, kc * 128:(kc + 1) * 128], u0[:, kc * 128:(kc + 1) * 128],
                                         ident[:]), [u0, ident], [pst])
    p.op("dve", lambda e: e.tensor_tensor(out=uT[:, :, t * 128:(t + 1) * 128],
                                          in0=ptv.rearrange("p (a b) -> p a b", a=8), in1=gB[:],
                                          op=ALU.mult), [pst, gB], [uT])


def load_gB(p, st, gvec_buf, gvec_ap):
    gT = p.sbuf([128, 8], F32, st)
    p.dma(gT, gT[:], gvec_buf, gvec_ap.rearrange("(c p) -> p c", p=128), allow_slow_non_contiguous=True)
    gB = p.sbuf([128, 8, 128], BF16, st)
    p.op("dve", lambda e: e.tensor_copy(out=gB[:], in_=gT[:].unsqueeze(2).to_broadcast([128, 8, 128])),
         [gT], [gB])
    return gB


def phase_norm(p, C, x_d, g_buf, g_ap, uT_d):
    with Phase(p) as st:
        ident = p.sbuf([128, 128], BF16, st)
        p.dma(ident, ident[:], C["ident"], C["ident"][:])
        gB = load_gB(p, st, g_buf, g_ap)
        uT = p.sbuf([128, 8, NT], BF16, st)
        xs = Rot([p.sbuf([128, DM], F32, st) for _ in range(3)])
        u0s = Rot([p.sbuf([128, DM], BF16, st) for _ in range(2)])
        junks = Rot([p.sbuf([128, DM], BF16, st) for _ in range(2)])
        sss = Rot([p.sbuf([128, 1], F32, st) for _ in range(3)])
        rstds = Rot([p.sbuf([128, 1], F32, st) for _ in range(3)])
        pss = Rot([p.psum([128, 512], F32, st) for _ in range(2)])
        for t in range(16):
            xb = xs.next()
            p.dma(xb, xb[:], x_d, x_d[t * 128:(t + 1) * 128, :])
            u0 = u0s.next()
            norm_chain(p, xb, xb[:], sss.next(), rstds.next(), junks.next(), u0)
            norm_T(p, u0, pss.next(), ident, gB, uT, t)
        p.dma(uT_d, uT_d[:], uT, uT[:])


def phase_attn(p, C, W, l, uT_d, oT_d, norm=None, uT_sb=None):
    with Phase(p) as st:
        ident = p.sbuf([128, 128], BF16, st)
        p.dma(ident, ident[:], C["ident"], C["ident"][:])
        ones = p.sbuf([128, 128], BF16, st)
        p.dma(ones, ones[:], C["ones"], C["ones"][:])
        masks = p.sbuf([128, 5, 128], BF16, st)
        p.dma(masks, masks[:], C["masks"], C["masks"][:])
        rope = p.sbuf([128, 3, 16, 32], F32, st)
        p.dma(rope, rope[:], C["rope"], C["rope"][:].rearrange("g p t c -> p g t c"))
        uT = uT_sb if uT_sb is not None else p.sbuf([128, 8, NT], BF16, st)
        if norm is None:
            p.dma(uT, uT[:], uT_d, uT_d[:])
        wq = p.sbuf([128, 8, 1536], BF16, st)
        stg = Rot([p.sbuf([128, 2, 512], F32, st) for _ in range(2)])
        QT = p.sbuf([128, 4, NT], BF16, st)
        KT = p.sbuf([128, 4, 20, 128], BF16, st)
        Vs = p.sbuf([128, 20, 512], BF16, st)
        OA = p.sbuf([128, 4, NT], F32, st)
        DA = p.sbuf([128, 4, NT], F32, st)
        qbs = Rot([p.sbuf([128, 512], BF16, st) for _ in range(6)])
        tAs = Rot([p.sbuf([128, 4, 32], F32, st) for _ in range(2)])
        tBs = Rot([p.sbuf([128, 4, 32], F32, st) for _ in range(2)])
        PTs = Rot([p.sbuf([128, 512], BF16, st) for _ in range(6)])
        ps = [p.psum([128, 512], F32, st) for _ in range(8)]
        p.op("pool", lambda e: e.memset(KT[:], 0.0), [], [KT])
        p.op("pool", lambda e: e.memset(Vs[:], 0.0), [], [Vs])
        w_in = W["w_in"]
        sc = 128.0 ** -0.5
        def group_w_pieces(g):
            out = []
            for blk in range(3):
                c0 = blk * 1536 + g * 512
                for k2 in range(4):
                    eng = ("pool", "dve")[len(out) % 2] if (g == 0 and norm is None) else "pool"
                    out.append(lambda blk=blk, k2=k2, c0=c0, eng=eng: load_cast(
                        p, stg, wq, wq[:, 2 * k2:2 * k2 + 2, blk * 512:(blk + 1) * 512], w_in,
                        w_in[l, k2 * 256:(k2 + 1) * 256, c0:c0 + 512].rearrange("(kc p) n -> p kc n", p=128),
                        lambda s: s[:], eng=eng))
            return out

        def load_group_w(g):
            for f in group_w_pieces(g):
                f()

        w0_pieces = group_w_pieces(0)
        if norm is None:
            for f in w0_pieces:
                f()
        if norm is not None:
            x_d, g_buf, g_ap = norm
            of = OA[:].rearrange("p a b -> p (a b)")
            df = DA[:].rearrange("p a b -> p (a b)")
            dfb = df.bitcast(BF16)
            xs_ = Rot([Buf(of[:, k * 1024:(k + 1) * 1024], "nx%d" % k) for k in range(3)])
            junks_ = Rot([Buf(of[:, 3072 + k * 1024:4096 + k * 1024], "nj%d" % k) for k in range(2)])
            u0s_ = Rot([Buf(dfb[:, k * 1024:(k + 1) * 1024], "nu%d" % k) for k in range(2)])
            gB = Buf(dfb[:, 2048:3072].rearrange("p (a b) -> p a b", a=8), "ngB")
            gT = Buf(df[:, 2048:2056], "ngT")
            sss_ = Rot([Buf(df[:, 2100 + k:2101 + k], "nss%d" % k) for k in range(3)])
            rstds_ = Rot([Buf(df[:, 2110 + k:2111 + k], "nrs%d" % k) for k in range(3)])
            for b_ in xs_.b:
                st.bufs.append(b_)
            st.bufs.append(gT)
            p.dma(gT, gT[:], g_buf, g_ap.rearrange("(c p) -> p c", p=128), allow_slow_non_contiguous=True)
            p.op("dve", lambda e: e.tensor_copy(out=gB[:], in_=gT[:].unsqueeze(2).to_broadcast([128, 8, 128])),
                 [gT], [gB])
            npss = Rot([ps[6], ps[7]])
            pend = None
            for t in range(16):
                xb = xs_.next()
                p.dma(xb, xb[:], x_d, x_d[t * 128:(t + 1) * 128, :])
                if t >= 2 and w0_pieces:
                    w0_pieces.pop(0)()
                u0 = u0s_.next()
                norm_chain(p, xb, xb[:], sss_.next(), rstds_.next(), junks_.next(), u0)
                if pend is not None:
                    norm_T(p, pend[0], npss.next(), ident, gB, uT, pend[1])
                pend = (u0, t)
            norm_T(p, pend[0], npss.next(), ident, gB, uT, pend[1])
            for f in w0_pieces:
                f()
            p.dma(uT_d, uT_d[:], uT, uT[:])
            p.barrier()
        for g, (D, T) in enumerate(GROUPS):

            def tokslice(jt):
                r, a = jt // T, jt % T
                base = r + D * 128 * a
                return r, a, slice(base, base + 127 * D + 1, D)

            qk_tiles = {}

            def proj_mm(jt):
                r, a, tok = tokslice(jt)
                pq, pk, pv = (ps[(jt % 2) * 3 + i] for i in range(3))
                for blk, pb in enumerate((pq, pk, pv)):
                    for kc in range(8):
                        p.op("pe", lambda e: e.matmul(pb[:], lhsT=uT[:, kc, tok],
                                                      rhs=wq[:, kc, blk * 512:(blk + 1) * 512],
                                                      start=(kc == 0), stop=(kc == 7)), [uT, wq], [pb])

            def proj_evac(jt):
                r, a, tok = tokslice(jt)
                pq, pk, pv = (ps[(jt % 2) * 3 + i] for i in range(3))
                c3 = rope[:, g, jt, 0:16].unsqueeze(1).to_broadcast([128, 4, 16])
                s3 = rope[:, g, jt, 16:32].unsqueeze(1).to_broadcast([128, 4, 16])
                qk = []
                for which, pb in enumerate((pq, pk)):
                    qb = qbs.next()
                    tA, tB = tAs.next(), tBs.next()
                    pv3 = pb[:].rearrange("p (h d) -> p h d", h=4)
                    qb3 = qb[:].rearrange("p (h d) -> p h d", h=4)
                    x1, x2 = pv3[:, :, 0:16], pv3[:, :, 16:32]
                    p.op("act", lambda e: e.activation(out=qb[:], in_=pb[:], func=AF.Copy), [pb], [qb])
                    p.op("dve", lambda e: e.tensor_tensor(out=tA[:, :, 0:16], in0=x1, in1=c3, op=ALU.mult), [pb, rope], [tA])
                    p.op("dve", lambda e: e.tensor_tensor(out=tA[:, :, 16:32], in0=x2, in1=c3, op=ALU.mult), [pb, rope], [tA])
                    p.op("dve", lambda e: e.tensor_tensor(out=tB[:, :, 0:16], in0=x1, in1=s3, op=ALU.mult), [pb, rope], [tB])
                    p.op("dve", lambda e: e.tensor_tensor(out=tB[:, :, 16:32], in0=x2, in1=s3, op=ALU.mult), [pb, rope], [tB])
                    p.op("dve", lambda e: e.tensor_tensor(out=qb3[:, :, 0:16], in0=tA[:, :, 0:16],
                                                          in1=tB[:, :, 16:32], op=ALU.subtract), [tA, tB], [qb])
                    p.op("dve", lambda e: e.tensor_tensor(out=qb3[:, :, 16:32], in0=tA[:, :, 16:32],
                                                          in1=tB[:, :, 0:16], op=ALU.add), [tA, tB], [qb])
                    qk.append(qb)
                qk_tiles[jt] = qk
                if T == 1:
                    p.op("act", lambda e: e.activation(out=Vs[:, jt, :], in_=pv[:], func=AF.Copy), [pv], [Vs])
                else:
                    s0 = r * (T + 1) + a
                    p.op("act", lambda e: e.activation(out=Vs[0:64, s0, :], in_=pv[0:64, :], func=AF.Copy),
                         [pv], [Vs])
                    p.op("act", lambda e: e.activation(out=Vs[64:128, s0 + 1, :], in_=pv[64:128, :], func=AF.Copy),
                         [pv], [Vs])

            def proj_T(jt):
                r, a, tok = tokslice(jt)
                for which, qb in enumerate(qk_tiles.pop(jt)):
                    pt = ps[6 + which]
                    ptv = pt[:].bitcast(BF16)
                    for h in range(4):
                        p.op("pe", lambda e: e.transpose(ptv[:, h * 128:(h + 1) * 128],
                                                         qb[:, h * 128:(h + 1) * 128], ident[:]),
                             [qb, ident], [pt])
                    src3 = ptv[:, 0:512].rearrange("p (h d) -> p h d", h=4)
                    if which == 0:
                        p.op("dve", lambda e: e.tensor_copy(out=QT[:, :, jt * 128:(jt + 1) * 128], in_=src3),
                             [pt], [QT])
                    elif T == 1:
                        p.op("act", lambda e: e.activation(out=KT[:, :, jt, :], in_=src3, func=AF.Copy), [pt], [KT])
                    else:
                        s0k = r * (T + 1) + a
                        p.op("act", lambda e: e.activation(out=KT[:, :, s0k, 0:64], in_=src3[:, :, 0:64],
                                                           func=AF.Copy), [pt], [KT])
                        p.op("act", lambda e: e.activation(out=KT[:, :, s0k + 1, 64:128], in_=src3[:, :, 64:128],
                                                           func=AF.Copy), [pt], [KT])

            proj_mm(0)
            proj_evac(0)
            proj_mm(1)
            proj_evac(1)
            for jt in range(2, 16):
                proj_mm(jt)
                proj_T(jt - 2)
                proj_evac(jt)
            proj_T(14)
            proj_T(15)
            if g + 1 < len(GROUPS):
                load_group_w(g + 1)

            stA = {}

            def coreA(jt):
                r, a, tok = tokslice(jt)
                bs = (jt % 2) * 4
                if T == 1:
                    blocks = [(jt, jt, 4)]
                else:
                    s0 = r * (T + 1) + a
                    blocks = [(s0, s0, 1 if a == 0 else 0), (s0 + 1, s0 + 1, 3 if a == T - 1 else 2)]
                PT = []
                for bi, (kslot, slot, mi) in enumerate(blocks):
                    pS = ps[bs + bi]
                    for h in range(4):
                        p.op("pe", lambda e: e.matmul(pS[:, h * 128:(h + 1) * 128], lhsT=KT[:, h, kslot, :],
                                                      rhs=QT[:, h, jt * 128:(jt + 1) * 128],
                                                      start=True, stop=True), [KT, QT], [pS])
                    pt_ = PTs.next()
                    p.op("act", lambda e: e.activation(out=pt_[:], in_=pS[:], func=AF.Exp, scale=sc), [pS], [pt_])
                    p3 = pt_[:].rearrange("p (h q) -> p h q", h=4)
                    p.op("dve", lambda e: e.tensor_tensor(
                        out=p3, in0=p3, in1=masks[:, mi, :].unsqueeze(1).to_broadcast([128, 4, 128]),
                        op=ALU.mult), [pt_, masks], [pt_])
                    PT.append(pt_)
                stA[jt] = (blocks, PT)

            def coreB(jt):
                r, a, tok = tokslice(jt)
                bs = (jt % 2) * 4
                blocks, PT = stA.pop(jt)
                pO, pD = ps[bs + 2], ps[bs + 3]
                nb_ = len(blocks)
                for h in range(4):
                    for bi, (kslot, slot, mi) in enumerate(blocks):
                        p.op("pe", lambda e: e.matmul(pO[:, h * 128:(h + 1) * 128],
                                                      lhsT=Vs[:, slot, h * 128:(h + 1) * 128],
                                                      rhs=PT[bi][:, h * 128:(h + 1) * 128],
                                                      start=(bi == 0), stop=(bi == nb_ - 1)), [Vs, PT[bi]], [pO])
                for bi in range(nb_):
                    p.op("pe", lambda e: e.matmul(pD[:], lhsT=ones[:], rhs=PT[bi][:],
                                                  start=(bi == 0), stop=(bi == nb_ - 1)), [ones, PT[bi]], [pD])
                o3 = pO[:].rearrange("p (h q) -> p h q", h=4)
                d3 = pD[:].rearrange("p (h q) -> p h q", h=4)
                if g == 0:
                    p.op("dve", lambda e: e.tensor_copy(out=OA[:, :, tok], in_=o3), [pO], [OA])
                    p.op("act", lambda e: e.activation(out=DA[:, :, tok], in_=d3, func=AF.Copy), [pD], [DA])
                else:
                    p.op("dve", lambda e: e.tensor_tensor(out=OA[:, :, tok], in0=o3, in1=OA[:, :, tok],
                                                          op=ALU.add), [pO, OA], [OA])
                    p.op("dve", lambda e: e.tensor_tensor(out=DA[:, :, tok], in0=d3, in1=DA[:, :, tok],
                                                          op=ALU.add), [pD, DA], [DA])

            coreA(0)
            for jt in range(1, 16):
                coreA(jt)
                coreB(jt - 1)
            coreB(15)
        def alias(b, name):
            a = Buf(b.t, name)
            a.writers, a.readers = dict(b.writers), dict(b.readers)
            return a

        for h in range(4):
            DAh, QTh = alias(DA, "DAh%d" % h), alias(QT, "QTh%d" % h)
            p.op("act", lambda e: e.activation(out=DA[:, h, :], in_=DA[:, h, :], func=AF.Ln), [DAh], [DAh])
            p.op("act", lambda e: e.activation(out=DA[:, h, :], in_=DA[:, h, :], func=AF.Exp, scale=-1.0),
                 [DAh], [DAh])
            p.op("dve", lambda e: e.tensor_tensor(out=QT[:, h, :], in0=OA[:, h, :], in1=DA[:, h, :],
                                                  op=ALU.mult), [OA, DAh], [QTh])
            p.dma(oT_d, oT_d[:, h, :], QTh, QT[:, h, :])


def phase_hyproj(p, C, W, l, uT_d, hv_d, hx1_d, hx2_d, uT_sb=None):
    with Phase(p) as st:
        ident = p.sbuf([128, 128], BF16, st)
        p.dma(ident, ident[:], C["ident"], C["ident"][:])
        if uT_sb is not None:
            uT = uT_sb
        else:
            uT = p.sbuf([128, 8, NT], BF16, st)
            p.dma(uT, uT[:], uT_d, uT_d[:])
        cw = p.sbuf([128, 24, 3], F32, st)
        for j in range(3):
            p.dma(cw, cw[:, :, j], W["conv_w"], W["conv_w"][l, j].rearrange("(c p) -> p c", p=128),
                  allow_slow_non_contiguous=True)
        cb = p.sbuf([128, 24], F32, st)
        p.dma(cb, cb[:], W["conv_b"], W["conv_b"][l].rearrange("(c p) -> p c", p=128),
              allow_slow_non_contiguous=True)
        toks = [p.sbuf([128, 16, DM], BF16, st) for _ in range(2)]
        stg = Rot([p.sbuf([128, 8, 128], F32, st) for _ in range(3)])
        wcs = Rot([p.sbuf([128, 8, 128], BF16, st) for _ in range(3)])
        hzs = Rot([p.sbuf([128, NT], F32, st) for _ in range(2)])
        hzbs = Rot([p.sbuf([128, NT], BF16, st) for _ in range(3)])
        pss = Rot([p.psum([128, NT], F32, st) for _ in range(2)])
        w_in = W["w_in"]
        stt = {}

        def hmm(cc):
            c0 = 4608 + cc * 128
            wc = wcs.next()
            load_cast(p, stg, wc, wc[:], w_in, w_in[l, :, c0:c0 + 128].rearrange("(kc p) n -> p kc n", p=128),
                      lambda s: s[:])
            pb = pss.next()
            stt[cc] = (pb, wc)
            hmm_part(cc, (0, 1))

        def hmm_part(cc, nbs):
            pb, wc = stt[cc]
            for nb in nbs:
                for kc in range(8):
                    p.op("pe", lambda e: e.matmul(pb[:, nb * 512:(nb + 1) * 512], lhsT=wc[:, kc, :],
                                                  rhs=uT[:, kc, nb * 512:(nb + 1) * 512],
                                                  start=(kc == 0), stop=(kc == 7)), [wc, uT], [pb])

        def hevac(cc):
            pb = stt[cc][0]
            hz, hzb = hzs.next(), hzbs.next()
            p.op("act", lambda e: e.activation(out=hz[:], in_=pb[:], func=AF.Identity,
                                               scale=cw[:, cc, 1:2], bias=cb[:, cc:cc + 1]), [pb, cw, cb], [hz])
            p.op("dve", lambda e: e.scalar_tensor_tensor(out=hz[:, 1:NT], in0=pb[:, 0:NT - 1],
                                                         scalar=cw[:, cc, 0:1], in1=hz[:, 1:NT],
                                                         op0=ALU.mult, op1=ALU.add), [pb, cw, hz], [hz])
            p.op("dve", lambda e: e.scalar_tensor_tensor(out=hzb[:, 0:NT - 1], in0=pb[:, 1:NT],
                                                         scalar=cw[:, cc, 2:3], in1=hz[:, 0:NT - 1],
                                                         op0=ALU.mult, op1=ALU.add), [pb, cw, hz], [hzb])
            p.op("dve", lambda e: e.tensor_copy(out=hzb[:, NT - 1:NT], in_=hz[:, NT - 1:NT]), [hz], [hzb])
            stt[cc] = (pb, hzb)

        def hT(cc):
            pb, hzb = stt.pop(cc)
            if cc < 16:
                tk = toks[cc // 8]
                c8 = cc % 8
                ptv = pb[:].bitcast(BF16)
                for t in range(16):
                    p.op("pe", lambda e: e.transpose(ptv[:, t * 128:(t + 1) * 128], hzb[:, t * 128:(t + 1) * 128],
                                                     ident[:]), [hzb, ident], [pb])
                src3 = ptv[:, 0:2048].rearrange("p (t c) -> p t c", t=16)
                p.op("act", lambda e: e.activation(out=tk[:, :, c8 * 128:(c8 + 1) * 128], in_=src3,
                                                   func=AF.Copy), [pb], [tk])
                if cc == 7:
                    p.dma(hv_d, hv_d[:], toks[0], toks[0][:], q="act")
                if cc == 15:
                    p.dma(hx1_d, hx1_d[:], toks[1], toks[1][:], q="act")
            else:
                p.dma(hx2_d, hx2_d[:, cc - 16, :], hzb, hzb[:], q="act")

        hmm(0)
        hmm_part(0, (2, 3))
        hevac(0)
        for cc in range(1, 24):
            hmm(cc)
            hmm_part(cc, (2, 3))
            hT(cc - 1)
            hevac(cc)
        hT(23)


def sin_reduced(p, pre_buf, pre_ap, nrows, fr, fb, vbuf, sbuf_, out_ap, out_buf):
    p.op("dve", lambda e: e.tensor_scalar(out=vbuf[0:nrows, :], in0=pre_ap, scalar1=fr[0:nrows, 0:1],
                                          scalar2=fb[0:nrows, 0:1], op0=ALU.mult, op1=ALU.add),
         [pre_buf, fr, fb], [vbuf])
    p.op("act", lambda e: e.activation(out=sbuf_[0:nrows, :], in_=vbuf[0:nrows, :], func=AF.Sign), [vbuf], [sbuf_])
    p.op("dve", lambda e: e.scalar_tensor_tensor(out=vbuf[0:nrows, :], in0=sbuf_[0:nrows, :], scalar=-math.pi,
                                                 in1=vbuf[0:nrows, :], op0=ALU.mult, op1=ALU.add),
         [sbuf_, vbuf], [vbuf])
    p.op("act", lambda e: e.activation(out=out_ap, in_=vbuf[0:nrows, :], func=AF.Sin, scale=-1.0),
         [vbuf], [out_buf])


def phase_filters(p, C, W, l, kab_d):
    def alias(b, name):
        a = Buf(b.t, name)
        a.writers, a.readers = dict(b.writers), dict(b.readers)
        a.is_psum = b.is_psum
        return a

    with Phase(p) as st:
        SD = [p.sbuf([128, 16, DM], BF16, st) for _ in range(4)]
        h2T = p.sbuf([64, NT], BF16, st)
        w3sd = p.sbuf([64, 4096], BF16, st)
        bias = p.sbuf([1, 2, DM], F32, st)
        p.dma(bias, bias[:], W["hyena_bias"], W["hyena_bias"][l:l + 1])
        decs = Rot([p.sbuf([128, DM], F32, st) for _ in range(2)])
        tmp0 = p.sbuf([128, DM], F32, st)
        pm = p.psum([128, 4096], F32, st)
        H = [pm, Buf(pm.t, "pm_hi")]
        H[1].is_psum = True
        with Phase(p) as s2:
            zT = p.sbuf([33, NT], F32, s2)
            p.dma(zT, zT[:], C["zT"], C["zT"][:])
            w1 = p.sbuf([33, 64], F32, s2)
            p.dma(w1, w1[:], W["filt_w1"], W["filt_w1"][l])
            w2 = p.sbuf([64, 64], F32, s2)
            p.dma(w2, w2[:], W["filt_w2"], W["filt_w2"][l])
            vecs = {}
            for nm in ("filt_b1", "filt_freq1", "filt_b2", "filt_freq2"):
                v = p.sbuf([64, 1], F32, s2)
                p.dma(v, v[:], W[nm], W[nm][l].rearrange("(p o) -> p o", o=1))
                vecs[nm] = v
            w3f = p.sbuf([64, 2048], F32, s2)
            w3 = p.sbuf([64, 4096], BF16, s2)
            for hh in range(2):
                p.dma(w3f, w3f[:], W["filt_w3"], W["filt_w3"][l, :, hh * 2048:(hh + 1) * 2048])
                p.op("pool", lambda e: e.tensor_copy(out=w3[:, hh * 2048:(hh + 1) * 2048], in_=w3f[:]), [w3f], [w3])
            fb1 = p.sbuf([64, 1], F32, s2)
            fb2 = p.sbuf([64, 1], F32, s2)
            p.op("dve", lambda e: e.tensor_tensor(out=fb1[:], in0=vecs["filt_b1"][:], in1=vecs["filt_freq1"][:],
                                                  op=ALU.mult), [vecs["filt_b1"], vecs["filt_freq1"]], [fb1])
            p.op("dve", lambda e: e.tensor_tensor(out=fb2[:], in0=vecs["filt_b2"][:], in1=vecs["filt_freq2"][:],
                                                  op=ALU.mult), [vecs["filt_b2"], vecs["filt_freq2"]], [fb2])
            for o in range(2):
                f_, b_ = w3[:, (2 * o) * 1024:(2 * o + 1) * 1024], w3[:, (2 * o + 1) * 1024:(2 * o + 2) * 1024]
                p.op("dve", lambda e: e.tensor_tensor(out=w3sd[:, (2 * o) * 1024:(2 * o + 1) * 1024], in0=f_, in1=b_,
                                                      op=ALU.add), [w3], [w3sd])
                p.op("dve", lambda e: e.tensor_tensor(out=w3sd[:, (2 * o + 1) * 1024:(2 * o + 2) * 1024], in0=f_,
                                                      in1=b_, op=ALU.subtract), [w3], [w3sd])
            h1T = p.sbuf([64, NT], F32, s2)
            vbs = Rot([p.sbuf([64, 512], F32, s2) for _ in range(2)])
            sbs = Rot([p.sbuf([64, 512], F32, s2) for _ in range(2)])
            for nb in range(4):
                sl = slice(nb * 512, (nb + 1) * 512)
                pq = H[nb % 2]
                c0 = (nb % 2) * 2048
                p.op("pe", lambda e: e.matmul(pm[0:64, c0:c0 + 512], lhsT=w1[:], rhs=zT[:, sl], start=True, stop=True),
                     [w1, zT], [pq])
                sin_reduced(p, pq, pm[0:64, c0:c0 + 512], 64, vecs["filt_freq1"], fb1, vbs.next(), sbs.next(),
                            h1T[:, sl], h1T)
            for nb in range(4):
                sl = slice(nb * 512, (nb + 1) * 512)
                pq = H[nb % 2]
                c0 = (nb % 2) * 2048
                p.op("pe", lambda e: e.matmul(pm[0:64, c0:c0 + 512], lhsT=w2[:], rhs=h1T[:, sl], start=True, stop=True),
                     [w2, h1T], [pq])
                sin_reduced(p, pq, pm[0:64, c0:c0 + 512], 64, vecs["filt_freq2"], fb2, vbs.next(), sbs.next(),
                            h2T[:, sl], h2T)

        def gen(t, o, half):
            dec = decs.next()
            p.dma(dec, dec[:], C["decay"], C["decay"][:, t, :])
            hb = H[half]
            ph_ = pm[:, half * 2048:(half + 1) * 2048]
            for cbk in range(4):
                col = o * 2048 + cbk * 512
                p.op("pe", lambda e: e.matmul(ph_[:, cbk * 512:(cbk + 1) * 512], lhsT=h2T[:, t * 128:(t + 1) * 128],
                                              rhs=w3sd[:, col:col + 512], start=True, stop=True), [h2T, w3sd], [hb])
            if t == 0:
                p.op("dve", lambda e: e.tensor_tensor(out=tmp0[:], in0=ph_[:, 0:1024], in1=dec[:], op=ALU.mult),
                     [hb, dec], [tmp0])
                p.op("dve", lambda e: e.tensor_tensor(out=tmp0[0:1, :], in0=tmp0[0:1, :], in1=bias[0:1, o, :],
                                                      op=ALU.add), [tmp0, bias], [tmp0])
                p.op("dve", lambda e: e.tensor_copy(out=SD[2 * o][:, t, :], in_=tmp0[:]), [tmp0], [SD[2 * o]])
            else:
                p.op("dve", lambda e: e.tensor_tensor(out=SD[2 * o][:, t, :], in0=ph_[:, 0:1024], in1=dec[:],
                                                      op=ALU.mult), [hb, dec], [SD[2 * o]])
            p.op("dve", lambda e: e.tensor_tensor(out=SD[2 * o + 1][:, t, :], in0=ph_[:, 1024:2048], in1=dec[:],
                                                  op=ALU.mult), [hb, dec], [SD[2 * o + 1]])

        for t in range(16):
            gen(t, 0, t % 2)

        ffc = Rot([p.sbuf([128, 16, 128], BF16, st) for _ in range(2)])
        ffs = Rot([p.sbuf([128, 16, 128], BF16, st) for _ in range(2)])
        outs = Rot([p.sbuf([128, DM], F32, st) for _ in range(4)])
        K = [alias(H[0], "pmK0"), alias(H[0], "pmK1")]

        def spec_block(k, fch, F_, unit, col0):
            pk = pm[:, col0:col0 + 1024]
            for cbk in range(2):
                for tc in range(16):
                    p.op("pe", lambda e: e.matmul(pk[:, cbk * 512:(cbk + 1) * 512], lhsT=F_[:, tc, :],
                                                  rhs=SD[k][:, tc, cbk * 512:(cbk + 1) * 512],
                                                  start=(tc == 0), stop=(tc == 15)), [F_, SD[k]], [unit])
            ob = outs.next()
            if k % 2 == 0:
                p.op("act", lambda e: e.activation(out=ob[:], in_=pk, func=AF.Copy), [unit], [ob])
            else:
                p.op("dve", lambda e: e.tensor_copy(out=ob[:], in_=pk), [unit], [ob])
            p.dma(kab_d, kab_d[k, fch], ob, ob[:], q="act")

        for fch in range(16):
            fc_, fs_ = ffc.next(), ffs.next()
            p.dma(fc_, fc_[:], C["ffc"], C["ffc"][fch])
            p.dma(fs_, fs_[:], C["ffs"], C["ffs"][fch])
            gen(fch, 1, 1)
            spec_block(0, fch, fc_, K[0], 0)
            spec_block(1, fch, fs_, K[1], 1024)
        K += [alias(H[1], "pmK2"), alias(H[1], "pmK3")]
        for fch in range(16):
            fc_, fs_ = ffc.next(), ffs.next()
            p.dma(fc_, fc_[:], C["ffc"], C["ffc"][fch])
            p.dma(fs_, fs_[:], C["ffs"], C["ffs"][fch])
            b0 = (fch % 2) * 2
            spec_block(2, fch, fc_, K[b0], b0 * 1024)
            spec_block(3, fch, fs_, K[b0 + 1], (b0 + 1) * 1024)


def phase_fft(p, C, kab_d, hv_d, hx1_d, hx2_d, z2T_d):
    with Phase(p) as st:
        xt = p.sbuf([128, 16, DM], BF16, st)
        p.dma(xt, xt[:], hv_d, hv_d[:])
        Y = p.sbuf([128, 32, DM], BF16, st)
        hx = p.sbuf([128, 16 * DM], BF16, st)
        ffc = Rot([p.sbuf([128, 16, 128], BF16, st) for _ in range(2)])
        ffs = Rot([p.sbuf([128, 16, 128], BF16, st) for _ in range(2)])
        kas = Rot([p.sbuf([128, DM], F32, st) for _ in range(2)])
        kbs = Rot([p.sbuf([128, DM], F32, st) for _ in range(2)])
        tmp = [p.sbuf([128, DM], F32, st) for _ in range(4)]
        fis = Rot([p.sbuf([128, 8, 512], BF16, st) for _ in range(2)])
        psA = [p.psum([128, DM], F32, st) for _ in range(2)]
        psB = [p.psum([128, DM], F32, st) for _ in range(2)]

        def forward(conv):
            for fch in range(16):
                if conv == 0 and fch == 2:
                    p.dma(hx, hx[:], hx1_d, hx1_d[:].rearrange("p a b -> p (a b)"))
                if conv == 1 and fch == 2:
                    p.dma(hx, hx[:], hx2_d, hx2_d[:].rearrange("p a b -> p (a b)"))
                fc_, fs_ = ffc.next(), ffs.next()
                p.dma(fc_, fc_[:], C["ffc"], C["ffc"][fch])
                p.dma(fs_, fs_[:], C["ffs"], C["ffs"][fch])
                ka, kb = kas.next(), kbs.next()
                p.dma(ka, ka[:], kab_d, kab_d[2 * conv, fch])
                p.dma(kb, kb[:], kab_d, kab_d[2 * conv + 1, fch])
                pa, pb = psA[fch % 2], psB[fch % 2]
                for F_, pp in ((fc_, pa), (fs_, pb)):
                    for cbk in range(2):
                        for tc in range(16):
                            p.op("pe", lambda e: e.matmul(pp[:, cbk * 512:(cbk + 1) * 512], lhsT=F_[:, tc, :],
                                                          rhs=xt[:, tc, cbk * 512:(cbk + 1) * 512],
                                                          start=(tc == 0), stop=(tc == 15)), [F_, xt], [pp])
                p.op("dve", lambda e: e.tensor_tensor(out=tmp[0][:], in0=pa[:], in1=ka[:], op=ALU.mult), [pa, ka], [tmp[0]])
                p.op("dve", lambda e: e.tensor_tensor(out=tmp[1][:], in0=pb[:], in1=kb[:], op=ALU.mult), [pb, kb], [tmp[1]])
                p.op("pool", lambda e: e.tensor_tensor(out=Y[:, fch, :], in0=tmp[0][:], in1=tmp[1][:], op=ALU.subtract),
                     [tmp[0], tmp[1]], [Y])
                p.op("dve", lambda e: e.tensor_tensor(out=tmp[2][:], in0=pa[:], in1=kb[:], op=ALU.mult), [pa, kb], [tmp[2]])
                p.op("dve", lambda e: e.tensor_tensor(out=tmp[3][:], in0=pb[:], in1=ka[:], op=ALU.mult), [pb, ka], [tmp[3]])
                p.op("pool", lambda e: e.tensor_tensor(out=Y[:, 16 + fch, :], in0=tmp[2][:], in1=tmp[3][:], op=ALU.add),
                     [tmp[2], tmp[3]], [Y])

        forward(0)
        banks = [psA[0], psB[0], psA[1], psB[1]]
        hx3 = hx[:].rearrange("p (a b) -> p a b", a=16)
        for nb in range(4):
            for fg in range(4):
                fi = fis.next()
                p.dma(fi, fi[:], C["finv"], C["finv"][nb, :, fg * 8:(fg + 1) * 8, :])
                order = ([(f8, tl) for f8 in range(8) for tl in range(4)] if fg < 3 else
                         [(f8, tl) for tl in range(4) for f8 in range(8)])
                for f8, tl in order:
                    fc = fg * 8 + f8
                    for cbk in range(2):
                        p.op("pe", lambda e: e.matmul(banks[tl][:, cbk * 512:(cbk + 1) * 512],
                                                      lhsT=fi[:, f8, tl * 128:(tl + 1) * 128],
                                                      rhs=Y[:, fc, cbk * 512:(cbk + 1) * 512],
                                                      start=(fc == 0), stop=(fc == 31)), [fi, Y], [banks[tl]])
            for tl in range(4):
                t = nb * 4 + tl
                p.op("dve", lambda e: e.tensor_tensor(out=xt[:, t, :], in0=banks[tl][:], in1=hx3[:, t, :],
                                                      op=ALU.mult), [banks[tl], hx], [xt])
        forward(1)
        hc3 = hx[:].rearrange("p (a b) -> p a b", a=8)
        for nb in range(4):
            for fg in range(4):
                fi = fis.next()
                p.dma(fi, fi[:], C["finv"], C["finv"][nb, :, fg * 8:(fg + 1) * 8, :])
                order = ([(f8, cc) for f8 in range(8) for cc in range(8)] if fg < 3 else
                         [(f8, cc) for cc in range(8) for f8 in range(8)])
                for f8, cc in order:
                    fc = fg * 8 + f8
                    bk = banks[cc // 2]
                    p.op("pe", lambda e: e.matmul(bk[:, (cc % 2) * 512:(cc % 2 + 1) * 512],
                                                  lhsT=Y[:, fc, cc * 128:(cc + 1) * 128], rhs=fi[:, f8, :],
                                                  start=(fc == 0), stop=(fc == 31)), [Y, fi], [bk])
            for cc in range(8):
                bk = banks[cc // 2]
                p.op("dve", lambda e: e.tensor_tensor(out=hc3[:, cc, nb * 512:(nb + 1) * 512],
                                                      in0=bk[:, (cc % 2) * 512:(cc % 2 + 1) * 512],
                                                      in1=hc3[:, cc, nb * 512:(nb + 1) * 512], op=ALU.mult),
                     [bk, hx], [hx])
            p.dma(z2T_d, z2T_d[:, :, nb * 512:(nb + 1) * 512], hx, hc3[:, :, nb * 512:(nb + 1) * 512], q="act")


def post_norm_residual(p, pacc, gpb, x_src_d, t, xs, junk, ss, rstd, tmpf, xnew):
    xb = xs.next()
    p.dma(xb, xb[:], x_src_d, x_src_d[t * 128:(t + 1) * 128, :])
    p.op("act", lambda e: e.activation(out=junk[:], in_=pacc[:], func=AF.Square, accum_out=ss[:]), [pacc], [junk, ss])
    rstd_from_ss(p, ss, rstd)
    p.op("dve", lambda e: e.scalar_tensor_tensor(out=tmpf[:], in0=pacc[:], scalar=rstd[:, 0:1], in1=gpb[:],
                                                 op0=ALU.mult, op1=ALU.mult), [pacc, rstd, gpb], [tmpf])
    p.op("pool", lambda e: e.tensor_tensor(out=xnew[:], in0=tmpf[:], in1=xb[:], op=ALU.add), [tmpf, xb], [xnew])


def phase_merge(p, C, W, l, uT_d, oT_d, z2T_d, x_src_d, xa_d, u2T_d):
    with Phase(p) as st:
        mT = p.sbuf([128, 8, NT], BF16, st)
        wo = p.sbuf([128, 8, DM], BF16, st)
        with Phase(p) as s2:
            stg = Rot([p.sbuf([128, 28, 128], F32, s2) for _ in range(2)])
            wbs = Rot([p.sbuf([128, 28, 128], BF16, s2) for _ in range(2)])

            def load_w(mc):
                cs = slice(mc * 128, (mc + 1) * 128)
                s_ = stg.next()
                wb_ = wbs.next()
                p.dma(s_, s_[:, 0:4, :], W["w_o_attn"], W["w_o_attn"][l, :, cs].rearrange("(k p) n -> p k n", p=128))
                p.dma(s_, s_[:, 4:12, :], W["w_o_hyena"], W["w_o_hyena"][l, :, cs].rearrange("(k p) n -> p k n", p=128))
                p.dma(s_, s_[:, 12:20, :], W["w_gate"], W["w_gate"][l, :, cs].rearrange("(k p) n -> p k n", p=128))
                p.dma(s_, s_[:, 20:28, :], W["w_gate"],
                      W["w_gate"][l, :, 1024 + mc * 128:1024 + (mc + 1) * 128].rearrange("(k p) n -> p k n", p=128))
                p.op("pool", lambda e: e.tensor_copy(out=wb_[:], in_=s_[:]), [s_], [wb_])
                return wb_

            wb_next = load_w(0)
            uTs = [p.sbuf([128, 8, 512], BF16, s2) for _ in range(4)]
            oTs = [p.sbuf([128, 4, 512], BF16, s2) for _ in range(4)]
            zTs = [p.sbuf([128, 8, 512], BF16, s2) for _ in range(4)]
            for nb in range(4):
                ns = slice(nb * 512, (nb + 1) * 512)
                p.dma(oTs[nb], oTs[nb][:], oT_d, oT_d[:, :, ns])
                p.dma(zTs[nb], zTs[nb][:], z2T_d, z2T_d[:, :, ns])
                p.dma(uTs[nb], uTs[nb][:], uT_d, uT_d[:, :, ns])
            bg = p.sbuf([128, 16], F32, s2)
            p.dma(bg, bg[:], W["b_gate"], W["b_gate"][l].rearrange("(c p) -> p c", p=128),
                  allow_slow_non_contiguous=True)
            sg = [Rot([p.sbuf([128, 512], F32, s2) for _ in range(2)]) for _ in range(2)]
            mm = [Rot([p.sbuf([128, 512], F32, s2) for _ in range(2)]) for _ in range(2)]
            pss = [p.psum([128, 512], F32, s2) for _ in range(8)]
            it = 0
            for mc in range(8):
                wb = wb_next
                if mc + 1 < 8:
                    wb_next = load_w(mc + 1)
                for nb in range(4):
                    ns = slice(nb * 512, (nb + 1) * 512)
                    pa, ph, pg0, pg1 = (pss[(it % 2) * 4 + i] for i in range(4))
                    it += 1
                    for k in range(4):
                        p.op("pe", lambda e: e.matmul(pa[:], lhsT=wb[:, k, :], rhs=oTs[nb][:, k, :], start=(k == 0),
                                                      stop=(k == 3)), [wb, oTs[nb]], [pa])
                    for k in range(8):
                        p.op("pe", lambda e: e.matmul(ph[:], lhsT=wb[:, 4 + k, :], rhs=zTs[nb][:, k, :], start=(k == 0),
                                                      stop=(k == 7)), [wb, zTs[nb]], [ph])
                    for k in range(8):
                        p.op("pe", lambda e: e.matmul(pg0[:], lhsT=wb[:, 12 + k, :], rhs=uTs[nb][:, k, :], start=(k == 0),
                                                      stop=(k == 7)), [wb, uTs[nb]], [pg0])
                    for k in range(8):
                        p.op("pe", lambda e: e.matmul(pg1[:], lhsT=wb[:, 20 + k, :], rhs=uTs[nb][:, k, :], start=(k == 0),
                                                      stop=(k == 7)), [wb, uTs[nb]], [pg1])
                    s0, s1 = sg[0].next(), sg[1].next()
                    m0, m1 = mm[0].next(), mm[1].next()
                    p.op("act", lambda e: e.activation(out=s0[:], in_=pg0[:], func=AF.Sigmoid, bias=bg[:, mc:mc + 1]),
                         [pg0, bg], [s0])
                    p.op("act", lambda e: e.activation(out=s1[:], in_=pg1[:], func=AF.Sigmoid,
                                                       bias=bg[:, 8 + mc:9 + mc]), [pg1, bg], [s1])
                    p.op("dve", lambda e: e.tensor_tensor(out=m0[:], in0=pa[:], in1=s0[:], op=ALU.mult), [pa, s0], [m0])
                    p.op("dve", lambda e: e.tensor_tensor(out=m1[:], in0=ph[:], in1=s1[:], op=ALU.mult), [ph, s1], [m1])
                    p.op("dve", lambda e: e.tensor_tensor(out=mT[:, mc, ns], in0=m0[:], in1=m1[:], op=ALU.add),
                         [m0, m1], [mT])
                if mc >= 4:
                    i = mc - 4
                    load_cast(p, stg, wo, wo[:, 2 * i:2 * i + 2, :], W["w_out"],
                              W["w_out"][l, i * 256:(i + 1) * 256, :].rearrange("(k p) n -> p k n", p=128),
                              lambda s_: s_[:].rearrange("p a b -> p (a b)")[:, 0:2048].rearrange("p (k n) -> p k n", k=2))
        ident = p.sbuf([128, 128], BF16, st)
        p.dma(ident, ident[:], C["ident"], C["ident"][:])
        gpb = p.sbuf([128, DM], F32, st)
        p.dma(gpb, gpb[:], W["norm_mix_post"], W["norm_mix_post"][l:l + 1, :].partition_broadcast(128))
        gB = load_gB(p, st, W["norm_ffn_pre"], W["norm_ffn_pre"][l])
        u2T = p.sbuf([128, 8, NT], BF16, st)
        xs = Rot([p.sbuf([128, DM], F32, st) for _ in range(2)])
        xns = Rot([p.sbuf([128, DM], F32, st) for _ in range(3)])
        u0s = Rot([p.sbuf([128, DM], BF16, st) for _ in range(4)])
        junks = Rot([p.sbuf([128, DM], BF16, st) for _ in range(2)])
        tmpfs = Rot([p.sbuf([128, DM], F32, st) for _ in range(2)])
        sss = Rot([p.sbuf([128, 1], F32, st) for _ in range(6)])
        rstds = Rot([p.sbuf([128, 1], F32, st) for _ in range(6)])
        pacc = Rot([p.psum([128, DM], F32, st) for _ in range(3)])
        pts = Rot([p.psum([128, 512], F32, st) for _ in range(2)])
        u0t = {}

        def omm(t):
            pa = pacc.next()
            for cbk in range(2):
                for mc in range(8):
                    p.op("pe", lambda e: e.matmul(pa[:, cbk * 512:(cbk + 1) * 512], lhsT=mT[:, mc, t * 128:(t + 1) * 128],
                                                  rhs=wo[:, mc, cbk * 512:(cbk + 1) * 512], start=(mc == 0),
                                                  stop=(mc == 7)), [mT, wo], [pa])
            u0t[t] = pa

        def ochain(t):
            pa = u0t[t]
            xn = xns.next()
            post_norm_residual(p, pa, gpb, x_src_d, t, xs, junks.next(), sss.next(), rstds.next(), tmpfs.next(), xn)
            p.dma(xa_d, xa_d[t * 128:(t + 1) * 128, :], xn, xn[:])
            u0 = u0s.next()
            norm_chain(p, xn, xn[:], sss.next(), rstds.next(), junks.next(), u0)
            u0t[t] = u0

        def oT(t):
            norm_T(p, u0t.pop(t), pts.next(), ident, gB, u2T, t)
            if t % 4 == 3:
                nb = t // 4
                p.dma(u2T_d, u2T_d[:, :, nb * 512:(nb + 1) * 512], u2T, u2T[:, :, nb * 512:(nb + 1) * 512], q="act")

        omm(0)
        ochain(0)
        omm(1)
        ochain(1)
        for t in range(2, 16):
            omm(t)
            oT(t - 2)
            ochain(t)
        oT(14)
        oT(15)


def phase_ffn(p, C, W, l, u2T_d, xa_d, xout_d):
    with Phase(p) as st:
      hT = p.sbuf([128, 22, NT], BF16, st)
      wd = p.sbuf([128, 22, DM], BF16, st)
      with Phase(p) as s1:
        u2Ts = [p.sbuf([128, 8, 512], BF16, s1) for _ in range(4)]
        stg = Rot([p.sbuf([128, 16, 128], F32, s1) for _ in range(2)])
        wbs = Rot([p.sbuf([128, 16, 128], BF16, s1) for _ in range(2)])
        sas = Rot([p.sbuf([128, 512], F32, s1) for _ in range(3)])
        pss = [p.psum([128, 512], F32, s1) for _ in range(4)]
        wgu = W["w_gate_up"]
        it = 0

        def load_gu(fc):
            s_ = stg.next()
            wb_ = wbs.next()
            p.dma(s_, s_[:, 0:8, :], wgu, wgu[l, :, fc * 128:(fc + 1) * 128].rearrange("(k p) n -> p k n", p=128))
            p.dma(s_, s_[:, 8:16, :], wgu,
                  wgu[l, :, DFF + fc * 128:DFF + (fc + 1) * 128].rearrange("(k p) n -> p k n", p=128))
            p.op("pool", lambda e: e.tensor_copy(out=wb_[:], in_=s_[:]), [s_], [wb_])
            return wb_

        wb_next = load_gu(0)
        for nb in range(4):
            p.dma(u2Ts[nb], u2Ts[nb][:], u2T_d, u2T_d[:, :, nb * 512:(nb + 1) * 512])
        for fc in range(22):
            wb = wb_next
            if fc + 1 < 22:
                wb_next = load_gu(fc + 1)
            for nb in range(4):
                ns = slice(nb * 512, (nb + 1) * 512)
                pa, pb = pss[(it % 2) * 2], pss[(it % 2) * 2 + 1]
                it += 1
                for k in range(8):
                    p.op("pe", lambda e: e.matmul(pa[:], lhsT=wb[:, k, :], rhs=u2Ts[nb][:, k, :], start=(k == 0),
                                                  stop=(k == 7)), [wb, u2Ts[nb]], [pa])
                for k in range(8):
                    p.op("pe", lambda e: e.matmul(pb[:], lhsT=wb[:, 8 + k, :], rhs=u2Ts[nb][:, k, :], start=(k == 0),
                                                  stop=(k == 7)), [wb, u2Ts[nb]], [pb])
                sa = sas.next()
                p.op("act", lambda e: e.activation(out=sa[:], in_=pa[:], func=AF.Silu), [pa], [sa])
                p.op("dve", lambda e: e.tensor_tensor(out=hT[:, fc, ns], in0=pb[:], in1=sa[:], op=ALU.mult),
                     [pb, sa], [hT])
            if fc % 2 == 1:
                i = fc // 2
                load_cast(p, stg, wd, wd[:, 2 * i:2 * i + 2, :], W["w_down"],
                          W["w_down"][l, i * 256:(i + 1) * 256, :].rearrange("(k p) n -> p k n", p=128),
                          lambda s_: s_[:].rearrange("p a b -> p (a b)").rearrange("p (k n) -> p k n", k=2))
      if True:
        with Phase(p) as s2:
            gpb = p.sbuf([128, DM], F32, s2)
            p.dma(gpb, gpb[:], W["norm_ffn_post"], W["norm_ffn_post"][l:l + 1, :].partition_broadcast(128))
            xs = Rot([p.sbuf([128, DM], F32, s2) for _ in range(2)])
            xns = Rot([p.sbuf([128, DM], F32, s2) for _ in range(2)])
            junks = Rot([p.sbuf([128, DM], BF16, s2) for _ in range(2)])
            tmpfs = Rot([p.sbuf([128, DM], F32, s2) for _ in range(2)])
            sss = Rot([p.sbuf([128, 1], F32, s2) for _ in range(3)])
            rstds = Rot([p.sbuf([128, 1], F32, s2) for _ in range(3)])
            pacc = Rot([p.psum([128, DM], F32, s2) for _ in range(2)])
            for t in range(16):
                pa = pacc.next()
                for cbk in range(2):
                    for fc in range(22):
                        p.op("pe", lambda e: e.matmul(pa[:, cbk * 512:(cbk + 1) * 512],
                                                      lhsT=hT[:, fc, t * 128:(t + 1) * 128],
                                                      rhs=wd[:, fc, cbk * 512:(cbk + 1) * 512], start=(fc == 0),
                                                      stop=(fc == 21)), [hT, wd], [pa])
                xn = xns.next()
                post_norm_residual(p, pa, gpb, xa_d, t, xs, junks.next(), sss.next(), rstds.next(), tmpfs.next(), xn)
                p.dma(xout_d, xout_d[t * 128:(t + 1) * 128, :], xn, xn[:])


_OUTER = [None, None]


def _outer_open(p):
    ph = Phase(p)
    ph.__enter__()
    _OUTER[0] = ph
    _OUTER[1] = p.sbuf([128, 8, NT], BF16, ph)


def _outer_close():
    _OUTER[0].__exit__(None, None, None)
    _OUTER[0] = None
    _OUTER[1] = None


def build(n_layers=2, stop_after=None, dbg=False, needed="auto"):
    if needed == "auto":
        p1 = build(n_layers, stop_after, dbg, needed=None)
        needed = {e: sorted(v) for e, v in p1.rec.items()}
        return build(n_layers, stop_after, dbg, needed=needed).nc
    p = Prog(needed)
    W = {k: p.dram(k, s, F32, kind="ExternalInput") for k, s in IN_SHAPES.items()}
    C = {k: p.dram("c_" + k, s, dt, kind="ExternalInput") for k, (s, dt) in CONST_SHAPES.items()}
    out = p.dram("out", [NT, DM], F32, kind="ExternalOutput")
    sk = "ExternalOutput" if dbg else "Internal"
    uT_d = p.dram("uT_d", [128, 8, NT], BF16, kind=sk)
    oT_d = p.dram("oT_d", [128, 4, NT], BF16, kind=sk)
    hv_d = p.dram("hv_d", [128, 16, DM], BF16, kind=sk)
    hx1_d = p.dram("hx1_d", [128, 16, DM], BF16, kind=sk)
    hx2_d = p.dram("hx2_d", [128, 8, NT], BF16, kind=sk)
    kab_d = p.dram("kab_d", [4, 16, 128, DM], F32, kind=sk)
    z2T_d = p.dram("z2T_d", [128, 8, NT], BF16, kind=sk)
    xa_d = p.dram("xa_d", [NT, DM], F32, kind=sk)
    xb_d = p.dram("xb_d", [NT, DM], F32, kind=sk)
    u2T_d = p.dram("u2T_d", [128, 8, NT], BF16, kind=sk)
    steps = []
    for l in range(n_layers):
        x_src = W["x"] if l == 0 else xb_d
        x_dst = out if l == n_layers - 1 else xb_d
        steps += [
            ("norm", lambda: _outer_open(p)),
            ("attn", lambda l=l, x_src=x_src: phase_attn(p, C, W, l, uT_d, oT_d,
                                                         norm=(x_src, W["norm_mix_pre"], W["norm_mix_pre"][l]),
                                                         uT_sb=_OUTER[1])),
            ("hyproj", lambda l=l: (phase_hyproj(p, C, W, l, uT_d, hv_d, hx1_d, hx2_d, uT_sb=_OUTER[1]),
                                    _outer_close())),
            ("filters", lambda l=l: phase_filters(p, C, W, l, kab_d)),
            ("fft", lambda l=l: phase_fft(p, C, kab_d, hv_d, hx1_d, hx2_d, z2T_d)),
            ("merge", lambda l=l, x_src=x_src: phase_merge(p, C, W, l, uT_d, oT_d, z2T_d, x_src, xa_d, u2T_d)),
            ("ffn", lambda l=l, x_dst=x_dst: phase_ffn(p, C, W, l, u2T_d, xa_d, x_dst)),
        ]
    for i, (name, fn) in enumerate(steps):
        fn()
        if stop_after is not None and i == stop_after:
            break
    if _OUTER[0] is not None:
        _OUTER[0].__exit__(None, None, None)
        _OUTER[0] = None
    p.finish()
    return p


_CONSTS = None


def kernel(**inputs):
    global _CONSTS
    if _CONSTS is None:
        _CONSTS = make_consts()
    nc = build()
    base = {k: np.ascontiguousarray(np.asarray(v, dtype=np.float32)) for k, v in inputs.items() if k != "x"}
    for k, v in _CONSTS.items():
        base["c_" + k] = v
    x = np.asarray(inputs["x"], dtype=np.float32)
    in_maps = []
    for b in range(8):
        m = dict(base)
        m["x"] = np.ascontiguousarray(x[b])
        in_maps.append(m)
    res = run_bass_kernel_spmd(nc, in_maps, core_ids=list(range(8)))
    return np.stack([np.asarray(r["out"], dtype=np.float32) for r in res.results], axis=0)
```
